# Optimizing a Trainium2 kernel written in Bass

```python
import math
import jax, jax.numpy as jnp
from jax import lax
import numpy as np

D_MODEL = 2048
BATCH = 2
SEQ = 8192
DEPTH = 1
DEC_BATCH = 8
DEC_SEQ = 64
PAST_LEN = 2048

CHUNK = 64
N_META = 16
W_A = 1024
CONV_A = 3
W_B = 1024
SSM_H = 16
SSM_G = W_B // SSM_H
SSM_P = 64
D_FF = 5632
CONV_F = 3
EPS = 1e-6
N_IN = 3 * W_A + W_B + 2 * D_MODEL
SPLITS = (W_A, 2 * W_A, 3 * W_A, 3 * W_A + W_B, 3 * W_A + W_B + D_MODEL)

kernel_name = "hybrid_shortconv_s5_convffn_stream_step"


def _rmsnorm(x, g):
    xf = x.astype(jnp.float32)
    y = xf * lax.rsqrt(jnp.mean(xf * xf, axis=-1, keepdims=True) + EPS)
    return (y * g.astype(jnp.float32)).astype(x.dtype)


def _causal_dwconv(v, buf, w):
    k = w.shape[0]
    t = v.shape[1]
    full = jnp.concatenate([buf.astype(v.dtype), v], axis=1)
    out = full[:, 0:t] * w[0]
    for i in range(1, k):
        out = out + full[:, i:i + t] * w[i]
    return out, full[:, t:]


def _linrec_combine(left, right):
    a_l, b_l = left
    a_r, b_r = right
    return a_l * a_r, a_r * b_l + b_r


def _ssm_chunk(h, u_c, abar, bbar, c, d):
    bu = jnp.einsum('blgh,gph->blgp', u_c.astype(jnp.complex64), bbar)
    bu = bu.at[:, 0].add(abar * h)
    a = jnp.broadcast_to(abar, bu.shape)
    _, hs = lax.associative_scan(_linrec_combine, (a, bu), axis=1)
    y = jnp.real(jnp.einsum('blgp,ghp->blgh', hs, c)) + d * u_c
    return hs[:, -1], y


def _ssm_mixer(u, h_re, h_im, lead, lam_re, lam_im, log_dt, b_re, b_im, c_re, c_im, d):
    f32 = jnp.float32
    lam = lax.complex(lam_re.astype(f32), lam_im.astype(f32))
    dt = jnp.exp(log_dt.astype(f32))[:, None]
    abar = jnp.exp(lam * dt)
    bmat = lax.complex(b_re.astype(f32), b_im.astype(f32))
    bbar = ((abar - 1.0) / lam)[..., None] * bmat
    cmat = lax.complex(c_re.astype(f32), c_im.astype(f32))
    dvec = d.astype(f32)
    uf = u.astype(f32)
    h = lax.complex(h_re.astype(f32), h_im.astype(f32))
    ys = []
    if lead > 0:
        h, y0 = _ssm_chunk(h, uf[:, :lead], abar, bbar, cmat, dvec)
        ys.append(y0)
    rest = uf[:, lead:]
    bsz, t = rest.shape[0], rest.shape[1]
    blk = min(CHUNK, t)
    n = t // blk
    chunks = rest.reshape(bsz, n, blk, SSM_G, SSM_H).swapaxes(0, 1)
    h, yc = lax.scan(lambda hh, uc: _ssm_chunk(hh, uc, abar, bbar, cmat, dvec), h, chunks)
    ys.append(yc.swapaxes(0, 1).reshape(bsz, t, SSM_G, SSM_H))
    y = jnp.concatenate(ys, axis=1) if len(ys) > 1 else ys[0]
    return y, jnp.real(h), jnp.imag(h)


def _layer(x, conv_buf, h_re, h_im, ffn_buf, lead, norm_mix_g, w_in, conv_a_w,
           lam_re, lam_im, log_dt, b_re, b_im, c_re, c_im, d, glu_w, glu_b,
           proj_a, proj_b, w_out, norm_ffn_g, w_up, ffn_conv_w, ffn_conv_b, w_down):
    bsz, t, _ = x.shape
    hn = _rmsnorm(x, norm_mix_g)
    z = hn @ w_in
    b_a, c_a, h_a, u_b, gate_a, gate_b = jnp.split(z, SPLITS, axis=-1)
    conv_out, new_conv_buf = _causal_dwconv(c_a * h_a, conv_buf, conv_a_w)
    out_a = b_a * conv_out
    y_b, new_re, new_im = _ssm_mixer(u_b.reshape(bsz, t, SSM_G, SSM_H), h_re, h_im, lead,
                                     lam_re, lam_im, log_dt, b_re, b_im, c_re, c_im, d)
    y_b = jax.nn.gelu(y_b.reshape(bsz, t, W_B)).astype(x.dtype)
    out_b = y_b * jax.nn.sigmoid(y_b @ glu_w + glu_b)
    merged = jax.nn.sigmoid(gate_a) * (out_a @ proj_a) + jax.nn.sigmoid(gate_b) * (out_b @ proj_b)
    x = x + merged @ w_out
    up = _rmsnorm(x, norm_ffn_g) @ w_up
    up, new_ffn_buf = _causal_dwconv(up, ffn_buf, ffn_conv_w)
    up = up + ffn_conv_b
    g, v = jnp.split(up, 2, axis=-1)
    x = x + (jax.nn.silu(g) * v) @ w_down
    return x, new_conv_buf, new_re, new_im, new_ffn_buf


def setup_inputs(seed: int = 0) -> dict:
    key = jax.random.key(seed)
    ks = jax.random.split(key, 32)
    f32 = jnp.float32
    nrm = lambda k, s, sc: jax.random.normal(k, s, f32) * sc
    lam_im = jnp.broadcast_to(math.pi * jnp.arange(SSM_P, dtype=f32), (DEPTH, SSM_G, SSM_P))
    return {
        "x_prompt": nrm(ks[0], (BATCH, SEQ, D_MODEL), 1.0),
        "x_sample": nrm(ks[1], (DEC_BATCH, DEC_SEQ, D_MODEL), 1.0),
        "cache_conv_a": nrm(ks[2], (DEPTH, DEC_BATCH, CONV_A - 1, W_A), 1.0),
        "state_ssm_re": nrm(ks[3], (DEPTH, DEC_BATCH, SSM_G, SSM_P), 0.1),
        "state_ssm_im": nrm(ks[4], (DEPTH, DEC_BATCH, SSM_G, SSM_P), 0.1),
        "cache_ffn_conv": nrm(ks[5], (DEPTH, DEC_BATCH, CONV_F - 1, 2 * D_FF), 1.0),
        "meta_tokens": nrm(ks[6], (N_META, D_MODEL), 1.0),
        "norm_mix_g": 1.0 + nrm(ks[7], (DEPTH, D_MODEL), 0.02),
        "w_in": nrm(ks[8], (DEPTH, D_MODEL, N_IN), D_MODEL ** -0.5),
        "conv_a_w": nrm(ks[9], (DEPTH, CONV_A, W_A), CONV_A ** -0.5),
        "ssm_lambda_re": -0.5 + nrm(ks[10], (DEPTH, SSM_G, SSM_P), 0.01),
        "ssm_lambda_im": lam_im + nrm(ks[11], (DEPTH, SSM_G, SSM_P), 0.01),
        "ssm_log_dt": jax.random.uniform(ks[12], (DEPTH, SSM_G), f32, math.log(1e-3), math.log(1e-1)),
        "ssm_b_re": nrm(ks[13], (DEPTH, SSM_G, SSM_P, SSM_H), (2 * SSM_H) ** -0.5),
        "ssm_b_im": nrm(ks[14], (DEPTH, SSM_G, SSM_P, SSM_H), (2 * SSM_H) ** -0.5),
        "ssm_c_re": nrm(ks[15], (DEPTH, SSM_G, SSM_H, SSM_P), (2 * SSM_P) ** -0.5),
        "ssm_c_im": nrm(ks[16], (DEPTH, SSM_G, SSM_H, SSM_P), (2 * SSM_P) ** -0.5),
        "ssm_d": nrm(ks[17], (DEPTH, SSM_G, SSM_H), 1.0),
        "glu_w": nrm(ks[18], (DEPTH, W_B, W_B), W_B ** -0.5),
        "glu_b": nrm(ks[19], (DEPTH, W_B), 0.02),
        "proj_a": nrm(ks[20], (DEPTH, W_A, D_MODEL), W_A ** -0.5),
        "proj_b": nrm(ks[21], (DEPTH, W_B, D_MODEL), W_B ** -0.5),
        "w_out": nrm(ks[22], (DEPTH, D_MODEL, D_MODEL), D_MODEL ** -0.5),
        "norm_ffn_g": 1.0 + nrm(ks[23], (DEPTH, D_MODEL), 0.02),
        "w_up": nrm(ks[24], (DEPTH, D_MODEL, 2 * D_FF), D_MODEL ** -0.5),
        "ffn_conv_w": nrm(ks[25], (DEPTH, CONV_F, 2 * D_FF), CONV_F ** -0.5),
        "ffn_conv_b": nrm(ks[26], (DEPTH, 2 * D_FF), 0.02),
        "w_down": nrm(ks[27], (DEPTH, D_FF, D_MODEL), D_FF ** -0.5),
        "norm_final_g": 1.0 + nrm(ks[28], (D_MODEL,), 0.02),
    }


def reference(x_prompt, x_sample, cache_conv_a, state_ssm_re, state_ssm_im, cache_ffn_conv,
              meta_tokens, norm_mix_g, w_in, conv_a_w, ssm_lambda_re, ssm_lambda_im, ssm_log_dt,
              ssm_b_re, ssm_b_im, ssm_c_re, ssm_c_im, ssm_d, glu_w, glu_b, proj_a, proj_b,
              w_out, norm_ffn_g, w_up, ffn_conv_w, ffn_conv_b, w_down, norm_final_g):
    bsz = x_prompt.shape[0]
    dt = x_prompt.dtype
    xp = jnp.concatenate([jnp.broadcast_to(meta_tokens.astype(dt), (bsz, N_META, D_MODEL)), x_prompt], axis=1)
    xs = x_sample
    zero_conv = jnp.zeros((bsz, CONV_A - 1, W_A), dt)
    zero_h = jnp.zeros((bsz, SSM_G, SSM_P), jnp.float32)
    zero_ffn = jnp.zeros((bsz, CONV_F - 1, 2 * D_FF), dt)
    p_conv, p_re, p_im, p_ffn = [], [], [], []
    s_conv, s_re, s_im, s_ffn = [], [], [], []
    for l in range(DEPTH):
        w = (norm_mix_g[l], w_in[l], conv_a_w[l], ssm_lambda_re[l], ssm_lambda_im[l], ssm_log_dt[l],
             ssm_b_re[l], ssm_b_im[l], ssm_c_re[l], ssm_c_im[l], ssm_d[l], glu_w[l], glu_b[l],
             proj_a[l], proj_b[l], w_out[l], norm_ffn_g[l], w_up[l], ffn_conv_w[l], ffn_conv_b[l], w_down[l])
        xp, a1, a2, a3, a4 = _layer(xp, zero_conv, zero_h, zero_h, zero_ffn, N_META, *w)
        xs, b1, b2, b3, b4 = _layer(xs, cache_conv_a[l], state_ssm_re[l], state_ssm_im[l], cache_ffn_conv[l], 0, *w)
        p_conv.append(a1); p_re.append(a2); p_im.append(a3); p_ffn.append(a4)
        s_conv.append(b1); s_re.append(b2); s_im.append(b3); s_ffn.append(b4)
    y_prompt = _rmsnorm(xp[:, N_META:], norm_final_g)
    y_sample = _rmsnorm(xs, norm_final_g)
    return (y_prompt, y_sample,
            jnp.stack(p_conv), jnp.stack(p_re), jnp.stack(p_im), jnp.stack(p_ffn),
            jnp.stack(s_conv), jnp.stack(s_re), jnp.stack(s_im), jnp.stack(s_ffn))
```

```python
import numpy as np
from contextlib import ExitStack
import concourse.bass as bass
import concourse.mybir as mybir
from concourse.bass_utils import run_bass_kernel_spmd

F32 = mybir.dt.float32
BF16 = mybir.dt.bfloat16
ALU = mybir.AluOpType
AF = mybir.ActivationFunctionType

D = 2048
KT = 16
DFF = 5632
FT = 44
TILE_COLS = [432, 432, 432, 432, 400]
TILE_OFF = [0, 432, 864, 1296, 1728]
NTOK = 2128
NPR = 2064
SEGLEN = 2056
NTMAX = 432
NCHMAX = 54
EPS = 1e-6
SLAB = 6144
NB = 3
PACK = 5120
NPRM = 440
NPT = 15
NPRE = NPT * 432
SAFE_SAME_ENGINE = True


def tile_segs(i, prepass=False):
    n = TILE_COLS[i]
    if i < 4:
        return [(0, n, 'p')]
    if prepass:
        return [(0, 328, 'p')]
    return [(0, 336, 'p'), (336, 400, 's')]


class Op:
    __slots__ = ('eng', 'fn', 'deps', 'dma', 'semi', 'val', 'idx', 'marked', 'count')


class Sched:
    ENG = ['pe', 'act', 'dve', 'pool', 'sp']

    def __init__(self):
        self.ops = {e: [] for e in self.ENG}
        self.lw = {}
        self.rd = {}
        self.fence = []
        self.rings = {}

    def ring(self, name, k, inc=16):
        self.rings[name] = {'k': k, 'n': 0, 'last': [None] * k, 'inc': inc}

    def add(self, eng, fn, reads=(), writes=(), dma=None, extra=()):
        op = Op()
        op.eng = eng; op.fn = fn; op.dma = dma; op.marked = False; op.count = 0
        op.semi = None; op.val = 0
        deps = list(extra) + list(self.fence)
        for k in reads:
            w = self.lw.get(k)
            if w is not None:
                deps.append(w)
        for k in writes:
            w = self.lw.get(k)
            if w is not None:
                deps.append(w)
            r = self.rd.get(k)
            if r:
                deps.extend(r[0].values())
                deps.extend(r[1])
        if dma is not None:
            rg = self.rings[dma]
            i = rg['n'] % rg['k']
            op.semi = (dma, i)
            op.val = rg['inc'] * (rg['n'] // rg['k'] + 1)
            if rg['last'][i] is not None:
                deps.append(rg['last'][i])
            rg['last'][i] = op
            rg['n'] += 1
        by_eng = {}
        by_dma = {}
        for d in deps:
            if d is op:
                continue
            if d.dma is not None:
                o = by_dma.get(d.semi)
                if o is None or o.val < d.val:
                    by_dma[d.semi] = d
            else:
                o = by_eng.get(d.eng)
                if o is None or o.idx < d.idx:
                    by_eng[d.eng] = d
        op.deps = list(by_eng.values()) + list(by_dma.values())
        op.idx = len(self.ops[eng])
        self.ops[eng].append(op)
        for k in writes:
            self.lw[k] = op
            self.rd[k] = [{}, []]
        for k in reads:
            r = self.rd.get(k)
            if r is None:
                r = self.rd[k] = [{}, []]
            if dma is not None:
                r[1].append(op)
            else:
                r[0][eng] = op
        return op

    def set_fence(self):
        f = []
        for e in ('act', 'dve', 'pe'):
            if self.ops[e]:
                f.append(self.ops[e][-1])
        for e in self.ENG:
            for o in self.ops[e]:
                if o.dma is not None and o.dma != 'cv':
                    f.append(o)
        self.fence = f

    def finalize(self):
        for e in self.ENG:
            for op in self.ops[e]:
                for d in op.deps:
                    if d.dma is None and (d.eng != op.eng or (SAFE_SAME_ENGINE and d.eng != 'pe')):
                        d.marked = True
        for e in self.ENG:
            c = 0
            for op in self.ops[e]:
                if op.marked:
                    c += 1
                op.count = c

    def emit(self, nc, block, engsem, dmasem):
        self.finalize()
        sched = self

        def run(engname, eng):
            waited = {}
            for op in sched.ops[engname]:
                for d in op.deps:
                    if d.dma is not None:
                        key = ('dma',) + d.semi
                        val = d.val
                        sem = dmasem[d.semi]
                    else:
                        if d.eng == engname and not (SAFE_SAME_ENGINE and engname != 'pe'):
                            continue
                        key = ('eng', d.eng)
                        val = d.count
                        sem = engsem[d.eng]
                    if waited.get(key, 0) < val:
                        eng.wait_ge(sem, val)
                        waited[key] = val
                ins = op.fn(eng)
                if op.marked:
                    ins.then_inc(engsem[engname], 1)
                if op.dma is not None:
                    ins.then_inc(dmasem[op.semi], sched.rings[op.dma]['inc'])

        @block.tensor
        def _(e):
            run('pe', e)

        @block.scalar
        def _(e):
            run('act', e)

        @block.vector
        def _(e):
            run('dve', e)

        @block.gpsimd
        def _(e):
            run('pool', e)

        @block.sync
        def _(e):
            run('sp', e)


def weight_items():
    it = []
    for j in range(8):
        it.append(('w_in', 16, 1024 + 128 * j, 'c%d' % j))
        it.append(('w_in', 16, 2048 + 128 * j, 'h%d' % j))
        it.append(('w_in', 16, 128 * j, 'b%d' % j))
    for j in range(8):
        it.append(('w_in', 16, 3072 + 128 * j, 'u%d' % j))
    for j in range(8):
        it.append(('glu_w', 8, 128 * j, 'glu%d' % j))
    for j in range(16):
        it.append(('proj_a', 8, 128 * j, 'pa%d' % j))
        it.append(('proj_b', 8, 128 * j, 'pb%d' % j))
        it.append(('w_in', 16, 4096 + 128 * j, 'ga%d' % j))
        it.append(('w_in', 16, 6144 + 128 * j, 'gb%d' % j))
    for j in range(16):
        it.append(('w_out', 16, 128 * j, 'wo%d' % j))
    for j in range(FT):
        it.append(('w_up', 16, 128 * j, 'ug%d' % j))
        it.append(('w_up', 16, DFF + 128 * j, 'uv%d' % j))
    for j in range(16):
        it.append(('w_down', FT, 128 * j, 'wd%d' % j))
    slabs = []
    cur = []
    used = 0
    pos = {}
    for x in it:
        sz = x[1] * 128
        if used + sz > SLAB:
            slabs.append((cur, used))
            cur = []
            used = 0
        pos[x[3]] = (len(slabs), used)
        cur.append(x)
        used += sz
    slabs.append((cur, used))
    return it, slabs, pos


ITEMS, SLABS, IPOS = weight_items()
NSLAB = len(SLABS)
PRE_ITEMS = ['u%d' % j for j in range(8)]


class Builder:
    def __init__(self, dbg=False, use_cc=True, ncores=8):
        self.dbg = dbg
        self.use_cc = use_cc
        self.ncores = ncores
        self.nc = bass.Bass("TRN2", target_bir_lowering=False)
        self.S = Sched()
        self.es = ExitStack()
        self.uid = 0

    def din(self, name, shape, dt=F32):
        return self.nc.dram_tensor(name, list(shape), dt, kind="ExternalInput").ap()

    def dout(self, name, shape, dt=F32):
        return self.nc.dram_tensor(name, list(shape), dt, kind="ExternalOutput").ap()

    def dscr(self, name, shape, dt):
        return self.nc.dram_tensor(name, list(shape), dt).ap()

    def sb(self, stack, name, shape, dt):
        return stack.enter_context(self.nc.sbuf_tensor(name, list(shape), dt))

    def add(self, *a, **k):
        return self.S.add(*a, **k)

    def tt(self, e, out, a, b, op, r, w):
        return self.add(e, lambda g: g.tensor_tensor(out=out, in0=a, in1=b, op=op), r, w)

    def ts(self, e, out, a, s1, s2, op0, op1, r, w):
        if op1 is None:
            return self.add(e, lambda g: g.tensor_scalar(out=out, in0=a, scalar1=s1, scalar2=None, op0=op0), r, w)
        return self.add(e, lambda g: g.tensor_scalar(out=out, in0=a, scalar1=s1, scalar2=s2, op0=op0, op1=op1), r, w)

    def stt(self, out, a, s, b, op0, op1, r, w):
        return self.add('dve', lambda g: g.scalar_tensor_tensor(out=out, in0=a, scalar=s, in1=b, op0=op0, op1=op1), r, w)

    def cp(self, e, out, a, r, w):
        if e == 'act':
            return self.add(e, lambda g: g.activation(out=out, in_=a, func=AF.Copy), r, w)
        return self.add(e, lambda g: g.tensor_copy(out=out, in_=a), r, w)

    def actf(self, out, a, func, r, w, scale=1.0, bias=0.0):
        return self.add('act', lambda g: g.activation(out=out, in_=a, func=func, scale=scale, bias=bias), r, w)

    def mm(self, out, lhsT, rhs, start, stop, r, w, **kw):
        return self.add('pe', lambda g: g.matmul(out, lhsT, rhs, start=start, stop=stop, **kw), r, w)

    def dma(self, q, ring, out, in_, r, w, **kw):
        return self.add(q, lambda g: g.dma_start(out=out, in_=in_, **kw), r, w, dma=ring)


    def declare(self):
        d = {}
        d['xT'] = self.din('xT', [D, NTOK])
        for nm, shp in (('w_in', [D, 8192]), ('glu_w', [1024, 1024]), ('proj_a', [1024, D]),
                        ('proj_b', [1024, D]), ('w_out', [D, D]), ('w_up', [D, 2 * DFF]),
                        ('w_down', [DFF, D])):
            d[nm] = self.din(nm, shp)
        d['prm'] = self.din('prm', [128, NPRM])
        d['lamA'] = self.din('lamA', [128, 3, 32])
        d['BA'] = self.din('BA', [128, 2, 1024])
        d['CA'] = self.din('CA', [128, 2, 1024])
        d['cst'] = self.din('cst', [128, 256])
        d['xpre'] = self.din('xpre', [D, NPRE])
        d['smp'] = self.din('smp', [128, 16 + 176 + 64])
        d['yT'] = self.dout('yT', [D, NTOK])
        d['sto'] = self.dout('sto', [128, 512])
        d['wscr'] = self.dscr('wscr', [NSLAB, 128, SLAB], BF16)
        d['ssmw'] = self.dscr('ssmw', [8, 128, PACK], BF16)
        d['ssmr'] = self.dscr('ssmr', [8, 128, 3 * 4 * NCHMAX], F32)
        if self.dbg:
            d['dbg'] = self.dout('dbg', [128, 8192])
        self.d = d

    class Arr:
        __slots__ = ('ap', 'key')

        def __init__(self, ap, key):
            self.ap = ap
            self.key = key

    def palloc(self, P, F):
        a = Builder.Arr(self.pp[0:P, self.ppos:self.ppos + F], ('pp', self.ppos))
        self.ppos += F
        assert self.ppos <= self.ppcols
        return a

    def p_tt(self, o, a, b, op):
        self.tt('dve', o.ap, a.ap, b.ap, op, [a.key, b.key], [o.key])

    def p_ts(self, o, a, m, c):
        self.ts('dve', o.ap, a.ap, float(m), float(c), ALU.mult, ALU.add, [a.key], [o.key])

    def p_cmul(self, o, a, b, t):
        self.p_tt(t[0], a[0], b[0], ALU.mult)
        self.p_tt(t[1], a[1], b[1], ALU.mult)
        self.p_tt(o[0], t[0], t[1], ALU.subtract)
        self.p_tt(t[0], a[0], b[1], ALU.mult)
        self.p_tt(t[1], a[1], b[0], ALU.mult)
        self.p_tt(o[1], t[0], t[1], ALU.add)

    def p_csq(self, o, a, t):
        self.p_tt(t[0], a[0], a[0], ALU.mult)
        self.p_tt(t[1], a[1], a[1], ALU.mult)
        self.p_tt(o[0], t[0], t[1], ALU.subtract)
        self.p_tt(t[0], a[0], a[1], ALU.mult)
        self.p_ts(o[1], t[0], 2.0, 0.0)

    def ptab(self, P, F, lr, li, ldt):
        A = lambda: self.palloc(P, F)
        dt = A()
        self.actf(dt.ap, ldt.ap, AF.Exp, [ldt.key], [dt.key])
        x = A(); th = A()
        self.p_tt(x, lr, dt, ALU.mult)
        self.p_tt(th, li, dt, ALU.mult)
        t = (A(), A())
        mag = A()
        self.p_ts(mag, x, 1.0 / 6.0, 1.0)
        for c in (5, 4, 3, 2, 1):
            self.p_tt(t[0], mag, x, ALU.mult)
            self.p_ts(mag, t[0], 1.0 / c, 1.0)
        y = A(); z = A()
        self.p_ts(y, th, 1.0 / 16.0, 0.0)
        self.p_tt(z, y, y, ALU.mult)
        sn = A(); cs = A()
        self.p_ts(sn, z, -1.0 / 156.0, 1.0)
        for c in (110, 72, 42, 20, 6):
            self.p_tt(t[0], sn, z, ALU.mult)
            self.p_ts(sn, t[0], -1.0 / c, 1.0)
        self.p_tt(t[0], sn, y, ALU.mult)
        self.p_ts(sn, t[0], 1.0, 0.0)
        self.p_ts(cs, z, -1.0 / 132.0, 1.0)
        for c in (90, 56, 30, 12, 2):
            self.p_tt(t[0], cs, z, ALU.mult)
            self.p_ts(cs, t[0], -1.0 / c, 1.0)
        cur = (cs, sn)
        for _ in range(4):
            nxt = (A(), A())
            self.p_csq(nxt, cur, t)
            cur = nxt
        pw = {1: (A(), A())}
        self.p_tt(pw[1][0], mag, cur[0], ALU.mult)
        self.p_tt(pw[1][1], mag, cur[1], ALU.mult)
        for k in range(2, 9):
            pw[k] = (A(), A())
            self.p_cmul(pw[k], pw[k - 1], pw[1], t)
        nr = A(); den = A(); inv = A(); u1 = A()
        self.p_ts(nr, pw[1][0], 1.0, -1.0)
        self.p_tt(t[0], lr, lr, ALU.mult)
        self.p_tt(t[1], li, li, ALU.mult)
        self.p_tt(den, t[0], t[1], ALU.add)
        self.add('dve', lambda g: g.reciprocal(out=inv.ap, in_=den.ap), [den.key], [inv.key])
        cf = (A(), A())
        self.p_tt(t[0], nr, lr, ALU.mult)
        self.p_tt(t[1], pw[1][1], li, ALU.mult)
        self.p_tt(u1, t[0], t[1], ALU.add)
        self.p_tt(cf[0], u1, inv, ALU.mult)
        self.p_tt(t[0], pw[1][1], lr, ALU.mult)
        self.p_tt(t[1], nr, li, ALU.mult)
        self.p_tt(u1, t[0], t[1], ALU.subtract)
        self.p_tt(cf[1], u1, inv, ALU.mult)
        G = {0: cf}
        for k in range(1, 8):
            G[k] = (A(), A())
            self.p_cmul(G[k], pw[k], cf, t)
        return dict(pw=pw, G=G, mag=mag, t=t, A=A)

    def prologue(self):
        d = self.d
        st = ExitStack()
        sb = lambda n, s, t: self.sb(st, n, s, t)
        self.ppcols = 10240
        self.pp = sb('pp', [128, self.ppcols], F32)
        self.ppos = 0
        lamA = sb('lamA_s', [128, 3, 32], F32)
        BA = sb('BA_s', [128, 2, 1024], F32)
        CA = sb('CA_s', [128, 2, 1024], F32)
        NCA = sb('NCA_s', [128, 2, 1024], F32)
        cst = sb('cst_s', [128, 256], F32)
        W1st = self.W1ALL.rearrange("p j (a s c) -> p j a s c", a=2, s=8)
        W3st = sb('W3st', [128, 32, 2, 8, 32], BF16)
        KTst = sb('KTst', [128, 8, 8, 128], BF16)
        RT = sb('RT', [128, 3, 32, NCHMAX], F32)
        T1 = sb('T1', [128, 1024], F32)
        T2 = sb('T2', [128, 1024], F32)
        EA = [sb('EA%d' % i, [128, 2, 1024], F32) for i in range(1)]
        TK = sb('TK', [128, 128], F32)
        prm = self.prm
        for nm, t_, src in (('lamA', lamA, d['lamA']), ('BA', BA, d['BA']),
                            ('CA', CA, d['CA']), ('cst', cst, d['cst'])):
            self.dma('sp', 'misc', t_[:], src, [], [nm])
        la = [Builder.Arr(lamA[:, i, :], 'lamA') for i in range(3)]
        tA = self.ptab(128, 32, la[0], la[1], la[2])
        A = tA['A']; t = tA['t']; pw = tA['pw']
        ct = self.ctab
        ck = lambda i, n=32: Builder.Arr(ct[:, i * 32:i * 32 + n], ('ct', i))
        A8 = (ck(0), ck(1)); RHO = ck(2)
        self.cp('dve', A8[0].ap, pw[8][0].ap, [pw[8][0].key], [A8[0].key])
        self.cp('dve', A8[1].ap, pw[8][1].ap, [pw[8][1].key], [A8[1].key])
        m2 = A(); m4 = A(); irho = A()
        self.p_tt(m2, tA['mag'], tA['mag'], ALU.mult)
        self.p_tt(m4, m2, m2, ALU.mult)
        self.p_tt(RHO, m4, m4, ALU.mult)
        self.add('dve', lambda g: g.reciprocal(out=irho.ap, in_=RHO.ap), [RHO.key], [irho.key])
        U = (A(), A())
        self.p_tt(U[0], A8[0], irho, ALU.mult)
        self.p_tt(U[1], A8[1], irho, ALU.mult)
        self.add('dve', lambda g: g.memset(RT[:, 0, :, 0:1], 1.0), [], ['RT'])
        self.add('dve', lambda g: g.memset(RT[:, 1, :, 0:1], 0.0), [], ['RT'])
        n = 1
        Un = U
        while n < NCHMAX:
            m = min(n, NCHMAX - n)
            bc = lambda a: a.ap.unsqueeze(2).to_broadcast([128, 32, m])
            o_r = RT[:, 0, :, n:n + m]; o_i = RT[:, 1, :, n:n + m]
            i_r = RT[:, 0, :, 0:m]; i_i = RT[:, 1, :, 0:m]
            v1 = T1[:, 0:32 * m].rearrange("p (a c) -> p a c", c=m)
            v2 = T2[:, 0:32 * m].rearrange("p (a c) -> p a c", c=m)
            kk = ['RT', Un[0].key, Un[1].key, 'T1', 'T2']
            self.tt('dve', v1, i_r, bc(Un[0]), ALU.mult, kk, ['T1'])
            self.tt('dve', v2, i_i, bc(Un[1]), ALU.mult, kk, ['T2'])
            self.tt('dve', o_r, v1, v2, ALU.subtract, kk, ['RT'])
            self.tt('dve', v1, i_r, bc(Un[1]), ALU.mult, kk, ['T1'])
            self.tt('dve', v2, i_i, bc(Un[0]), ALU.mult, kk, ['T2'])
            self.tt('dve', o_i, v1, v2, ALU.add, kk, ['RT'])
            n += m
            if n < NCHMAX:
                nx = (A(), A())
                self.p_csq(nx, Un, t)
                Un = nx
        self.add('dve', lambda g: g.memset(RT[:, 2, :, 0:1], 0.0), [], ['RT'])
        self.cp('dve', RT[:, 2, :, 1:NCHMAX], RHO.ap.unsqueeze(2).to_broadcast([128, 32, NCHMAX - 1]), [RHO.key], ['RT'])
        for j in range(8):
            self.dma('sp', 'misc', d['ssmr'][j].rearrange("p (a r c) -> p a r c", a=3, r=4),
                     RT[:, :, 4 * j:4 * j + 4, :], ['RT'], ['ssmr'])
        self.rl = {}
        for idx, m in enumerate((54, 42, 8)):
            a = (ck(3 + 2 * idx), ck(4 + 2 * idx))
            for ri in range(2):
                self.cp('dve', a[ri].ap, RT[:, ri, :, m - 1], ['RT'], [a[ri].key])
            self.rl[m] = a
        self.A8 = A8; self.RHO = RHO
        self.ts('dve', NCA[:], CA[:], -1.0, None, ALU.mult, None, ['CA'], ['NCA'])
        bm = cst[:, 0:128]; ident = cst[:, 128:256]
        for k in range(8):
            e = EA[0]; ek = 'EA0'
            gr, gi = tA['G'][k]
            bc = lambda a: a.ap.unsqueeze(2).to_broadcast([128, 32, 32])
            v = lambda x: x.rearrange("p (a c) -> p a c", c=32)
            kk = [gr.key, gi.key, 'BA', 'T1', 'T2']
            self.tt('dve', v(T1[:]), v(BA[:, 0, :]), bc(gr), ALU.mult, kk, ['T1'])
            self.tt('dve', v(T2[:]), v(BA[:, 1, :]), bc(gi), ALU.mult, kk, ['T2'])
            self.tt('dve', e[:, 0, :], T1[:], T2[:], ALU.subtract, ['T1', 'T2'], [ek])
            self.tt('dve', v(T1[:]), v(BA[:, 0, :]), bc(gi), ALU.mult, kk, ['T1'])
            self.tt('dve', v(T2[:]), v(BA[:, 1, :]), bc(gr), ALU.mult, kk, ['T2'])
            self.tt('dve', e[:, 1, :], T1[:], T2[:], ALU.add, ['T1', 'T2'], [ek])
            for ri in range(2):
                for hf in range(2):
                    pi, ps = self.ps_get()
                    for jj in range(4):
                        j = 4 * hf + jj
                        self.mm(ps[:, 128 * jj:128 * jj + 128], e[:, ri, 128 * j:128 * j + 128], ident, True, True,
                                [ek, 'cst'], [('ps', pi)])
                    self.cp('act', W1st[:, 4 * hf:4 * hf + 4, ri, 7 - k, :],
                            ps[:, 0:512].rearrange("p (a c) -> p a c", c=128), [('ps', pi)], ['W1ALL'])
            for j in range(8):
                pi, ps = self.ps_get()
                sl = slice(128 * j, 128 * j + 128)
                self.mm(ps[:, 0:128], e[:, 0, sl], CA[:, 0, sl], True, False, [ek, 'CA'], [('ps', pi)])
                self.mm(ps[:, 0:128], e[:, 1, sl], NCA[:, 1, sl], False, True, [ek, 'NCA'], [('ps', pi)])
                if k == 0:
                    self.tt('dve', TK[:], ps[:, 0:128], bm, ALU.mult, [('ps', pi), 'cst'], ['TK'])
                    self.stt(KTst[:, j, k, :], ident, prm[:, 432 + j:433 + j], TK[:], ALU.mult, ALU.add,
                             ['TK', 'cst', 'prm'], ['KTst'])
                else:
                    self.tt('dve', KTst[:, j, k, :], ps[:, 0:128], bm, ALU.mult, [('ps', pi), 'cst'], ['KTst'])
        for k in range(1, 9):
            ar, ai = pw[k]
            bc = lambda a: a.ap.unsqueeze(2).to_broadcast([128, 32, 32])
            v = lambda x: x.rearrange("p (a c) -> p a c", c=32)
            kk = [ar.key, ai.key, 'CA', 'NCA', 'T1', 'T2']
            self.tt('dve', v(T1[:]), v(CA[:, 0, :]), bc(ar), ALU.mult, kk, ['T1'])
            self.tt('dve', v(T2[:]), v(CA[:, 1, :]), bc(ai), ALU.mult, kk, ['T2'])
            self.tt('dve', W3st[:, :, 0, k - 1, :], v(T1[:]), v(T2[:]), ALU.subtract, ['T1', 'T2'], ['W3st'])
            self.tt('dve', v(T1[:]), v(NCA[:, 0, :]), bc(ai), ALU.mult, kk, ['T1'])
            self.tt('dve', v(T2[:]), v(CA[:, 1, :]), bc(ar), ALU.mult, kk, ['T2'])
            self.tt('dve', W3st[:, :, 1, k - 1, :], v(T1[:]), v(T2[:]), ALU.subtract, ['T1', 'T2'], ['W3st'])
        sw = d['ssmw'].rearrange("j p e -> p j e")
        self.dma('sp', 'misc', sw[:, :, 0:2048], self.W1ALL, ['W1ALL'], ['ssmw'])
        self.dma('sp', 'misc', sw[:, :, 2048:4096],
                 W3st[:].rearrange("p (j r) a s c -> p j (r a s c)", r=4), ['W3st'], ['ssmw'])
        self.dma('sp', 'misc', sw[:, :, 4096:5120], KTst[:].rearrange("p j k c -> p j (k c)"), ['KTst'], ['ssmw'])
        self.S.set_fence()
        st.close()

    def xk(self, k):
        return ('X', self.xi, k)

    def set_x(self, i):
        self.xi = i % 2
        self.X = self.XB[self.xi]

    def ps_get(self):
        i = self.psn % 8
        self.psn += 1
        return i, self.ps[i]

    def ps_get4(self):
        while self.psn % 4 != 0:
            self.psn += 1
        i = self.psn % 8
        self.psn += 4
        return i, self.psg[i // 4]

    def rg(self, name):
        bufs = self.rings_sb[name]
        i = self.ring_n.get(name, 0)
        self.ring_n[name] = i + 1
        k = i % len(bufs)
        return bufs[k], (name, k)

    def ensure(self, slab):
        if slab in self.win:
            return self.win[slab]
        while True:
            s = self.wseq[self.wptr]
            self.issue_slab()
            if s == slab:
                return self.win[slab]

    def issue_slab(self):
        s = self.wseq[self.wptr]
        self.wptr += 1
        slot = self.wn % NB
        self.wn += 1
        for k in [k for k, v in self.win.items() if v == slot]:
            del self.win[k]
        used = SLABS[s][1]
        self.dma('sp', 'w', self.WR[slot][:, 0:used], self.d['wscr'][s][:, 0:used],
                 [('scr', s, x[3]) for x in SLABS[s][0]], [('wr', slot)])
        self.win[s] = slot

    def prefetch(self, k):
        for _ in range(k):
            if self.wptr < len(self.wseq) and len(self.win) < NB:
                self.issue_slab()

    def gemm(self, name, nk, rhs_fn, n, first=True, last=True, ps=None):
        slab, off = IPOS[name]
        slot = self.ensure(slab)
        if ps is None:
            ps = self.ps_get()
        pi, pt = ps
        for k in range(nk):
            lhsT = self.WR[slot][:, off + k * 128: off + (k + 1) * 128]
            rhs, rkey = rhs_fn(k)
            self.mm(pt[:, 0:n], lhsT, rhs, first and k == 0, last and k == nk - 1,
                    [('wr', slot), rkey], [('ps', pi)])
        return ps

    def convert_weights(self):
        d = self.d
        order = [x for x in ITEMS if x[3] in PRE_ITEMS] + [x for x in ITEMS if x[3] not in PRE_ITEMS]
        for (mat, nk, c0, nm) in order:
            slab, off = IPOS[nm]
            src = d[mat].rearrange("(k p) n -> p k n", p=128)[:, :, c0:c0 + 128]
            dst = d['wscr'][slab][:, off:off + nk * 128].rearrange("p (k c) -> p k c", c=128)
            self.dma('pool', 'cv', dst, src, [], [('scr', slab, nm)])

    def load_x(self, i, pre=False):
        if pre:
            n = 432; off = 432 * i
            xv = self.d['xpre'].rearrange("(k p) n -> p k n", p=128)
        else:
            n = TILE_COLS[i]; off = TILE_OFF[i]
            xv = self.d['xT'].rearrange("(k p) n -> p k n", p=128)
        for q in range(4):
            self.dma('sp', 'x', self.X[:, 4 * q:4 * q + 4, 0:n], xv[:, 4 * q:4 * q + 4, off:off + n],
                     [], [self.xk(k) for k in range(4 * q, 4 * q + 4)])

    def norm_stats(self, n):
        ps = self.ps_get()
        pi, pt = ps
        for k in range(16):
            sq, sk = self.rg('SQ')
            self.actf(sq[:, 0:n], self.X[:, k, 0:n], AF.Square, [self.xk(k)], [sk])
            self.mm(pt[:, 0:n], self.ONES[:], sq[:, 0:n], k == 0, k == 15, [sk, 'ONES'], [('ps', pi)])
        tf, tk = self.rg('TF')
        self.actf(tf[:, 0:n], pt[:, 0:n], AF.Sqrt, [('ps', pi)], [tk], scale=1.0 / D, bias=self.epsc[:, 0:1])
        rs = self.RSTD
        self.add('dve', lambda g: g.reciprocal(out=rs[:, 0:n], in_=tf[:, 0:n]), [tk], [self.rkey])

    def norm(self, n, out_fn):
        self.norm_stats(n)
        for k in range(16):
            out_fn(k)

    def norm_to_hn(self, n, gofs):
        def f(k):
            self.stt(self.HN[:, k, 0:n], self.X[:, k, 0:n], self.prm[:, gofs + k:gofs + k + 1],
                     self.RSTD[:, 0:n], ALU.mult, ALU.mult, [self.xk(k), self.rkey, 'prm'], [('HN', k)])
        self.norm(n, f)

    def hn_rhs(self, n):
        return lambda k: (self.HN[:, k, 0:n], ('HN', k))

    def ssm_begin(self, segs, prepass):
        for si, (c0, c1, st) in enumerate(segs):
            H = self.HIN[st]
            ci = self.CI[si]
            k = ('CI', si)
            hk = ('HIN', st)
            t1 = self.TS[:, 0, :]; t2 = self.TS[:, 1, :]
            A8 = self.A8
            rk = [hk, A8[0].key, A8[1].key, 'TS']
            self.tt('dve', t1, H[:, 0, :], A8[0].ap, ALU.mult, rk, ['TS'])
            self.tt('dve', t2, H[:, 1, :], A8[1].ap, ALU.mult, rk, ['TS'])
            self.tt('dve', ci[:, 0, :], t1, t2, ALU.subtract, ['TS'], [k])
            self.tt('dve', t1, H[:, 0, :], A8[1].ap, ALU.mult, rk, ['TS'])
            self.tt('dve', t2, H[:, 1, :], A8[0].ap, ALU.mult, rk, ['TS'])
            self.tt('dve', ci[:, 1, :], t1, t2, ALU.add, ['TS'], [k])

    def ssm_end(self, segs, prepass):
        for si, (c0, c1, st) in enumerate(segs):
            m = (c1 - c0) // 8
            rl = self.rl[m]
            ql = self.QL[si]
            H = self.HIN[st]
            t1 = self.TS[:, 0, :]; t2 = self.TS[:, 1, :]
            rk = [('QL', si), rl[0].key, rl[1].key, 'TS']
            hk = ('HIN', st)
            self.tt('dve', t1, ql[:, 0, :], rl[0].ap, ALU.mult, rk, ['TS'])
            self.tt('dve', t2, ql[:, 1, :], rl[1].ap, ALU.mult, rk, ['TS'])
            self.tt('dve', H[:, 0, :], t1, t2, ALU.subtract, ['TS'], [hk])
            self.tt('dve', t1, ql[:, 0, :], rl[1].ap, ALU.mult, rk, ['TS'])
            self.tt('dve', t2, ql[:, 1, :], rl[0].ap, ALU.mult, rk, ['TS'])
            self.tt('dve', H[:, 1, :], t1, t2, ALU.add, ['TS'], [hk])

    def ssm_tile(self, j, u, ukey, n, segs, prepass):
        d = self.d
        nch = n // 8
        pr, prk = self.rg('PR')
        if prepass:
            pk = None
            pkk = 'W1ALL'
        else:
            pk, pkk = self.rg('PK')
            self.dma('sp', 'pk', pk[:], d['ssmw'][j], ['ssmw'], [pkk])
        self.dma('sp', 'pk', pr[:], d['ssmr'][j].rearrange("p (a r c) -> p a r c", a=3, r=4), ['ssmr'], [prk])
        if prepass:
            W1v = self.W1ALL[:, j, :].rearrange("p (a s c) -> p a s c", a=2, s=8)
            W3v = KTv = None
        else:
            W1v = pk[:, 0:2048].rearrange("p (a s c) -> p a s c", a=2, s=8)
            W3v = pk[:, 2048:4096].rearrange("p (r a s c) -> p r a s c", r=4, a=2, s=8)
            KTv = pk[:, 4096:5120].rearrange("p (k c) -> p k c", k=8)
        uv = u[:, 0:n].rearrange("p (c s) -> p s c", s=8)
        pi1, pG = self.ps_get4()
        skeys = [('ps', pi1 + r) for r in range(4)]
        for ri in range(2):
            for s in range(8):
                for r in range(4):
                    o = r * 512 + ri * 256
                    self.mm(pG[:, o:o + nch], W1v[32 * r:32 * r + 32, ri, s, :], uv[32 * r:32 * r + 32, s, :],
                            s == 0, s == 7, [pkk, ukey], [skeys[r]], tile_position=(32 * r, 0))
        if self.ssm_lim <= 0:
            return None
        Sv = [pG[:].rearrange("p (r c) -> p r c", r=4)[:, :, ri * 256:ri * 256 + nch] for ri in range(2)]
        hp, hpk = self.rg('HP')
        XR, XI, QR, QI, TT = self.XR, self.XI, self.QR, self.QI, self.TT
        for si, (c0, c1, st) in enumerate(segs):
            ca, cb = c0 // 8, c1 // 8
            m = cb - ca
            Rr = pr[:, 0, :, 0:m]; Ri = pr[:, 1, :, 0:m]
            Sr = Sv[0][:, :, ca:cb]; Si = Sv[1][:, :, ca:cb]
            xr = XR[:, :, 0:m]; xi = XI[:, :, 0:m]; tt_ = TT[:, :, 0:m]
            self.tt('dve', xr, Sr, Rr, ALU.mult, skeys + [prk], ['XR'])
            self.tt('dve', tt_, Si, Ri, ALU.mult, skeys + [prk], ['TT'])
            self.tt('dve', xr, xr, tt_, ALU.add, ['XR', 'TT'], ['XR'])
            self.tt('dve', xi, Si, Rr, ALU.mult, skeys + [prk], ['XI'])
            self.tt('dve', tt_, Sr, Ri, ALU.mult, skeys + [prk], ['TT'])
            self.tt('dve', xi, xi, tt_, ALU.subtract, ['XI', 'TT'], ['XI'])
            ci = self.CI[si]
            self.tt('dve', XR[:, :, 0:1], XR[:, :, 0:1], ci[:, 0, 4 * j:4 * j + 4].unsqueeze(2), ALU.add,
                    ['XR', ('CI', si)], ['XR'])
            self.tt('dve', XI[:, :, 0:1], XI[:, :, 0:1], ci[:, 1, 4 * j:4 * j + 4].unsqueeze(2), ALU.add,
                    ['XI', ('CI', si)], ['XI'])
            if len(segs) == 1 and nch == NCHMAX and m == NCHMAX:
                fl = lambda t_: t_[:].rearrange("p r c -> p (r c)")
                dec = pr[:, 2, :, :].rearrange("p r c -> p (r c)")
                for (Q, X, qn, xn) in ((QR, XR, 'QR', 'XR'), (QI, XI, 'QI', 'XI')):
                    q_ = fl(Q); x_ = fl(X)
                    self.add('dve', lambda g, q_=q_, x_=x_, dec=dec: g.tensor_tensor_scan(
                        out=q_, data0=dec, data1=x_, initial=0.0, op0=ALU.mult, op1=ALU.add),
                        [xn, prk], [qn])
            else:
                for r in range(4):
                    rho = self.RHO.ap[:, 4 * j + r:4 * j + r + 1].to_broadcast([128, m])
                    for (Q, X, qn, xn) in ((QR, XR, 'QR', 'XR'), (QI, XI, 'QI', 'XI')):
                        q_ = Q[:, r, 0:m]; x_ = X[:, r, 0:m]
                        self.add('dve', lambda g, q_=q_, x_=x_, rho=rho: g.tensor_tensor_scan(
                            out=q_, data0=rho, data1=x_, initial=0.0, op0=ALU.mult, op1=ALU.add),
                            [xn, self.RHO.key], [qn])
            ql = self.QL[si]
            self.cp('act', ql[:, 0, 4 * j:4 * j + 4], QR[:, :, m - 1], ['QR'], [('QL', si)])
            self.cp('act', ql[:, 1, 4 * j:4 * j + 4], QI[:, :, m - 1], ['QI'], [('QL', si)])
            if prepass:
                continue
            H = self.HIN[st]
            self.cp('dve', hp[:, 0, :, ca], H[:, 0, 4 * j:4 * j + 4], [('HIN', st)], [hpk])
            self.cp('dve', hp[:, 1, :, ca], H[:, 1, 4 * j:4 * j + 4], [('HIN', st)], [hpk])
            if m > 1:
                mm1 = m - 1
                Rr1 = pr[:, 0, :, 0:mm1]; Ri1 = pr[:, 1, :, 0:mm1]
                qr = QR[:, :, 0:mm1]; qi = QI[:, :, 0:mm1]
                t1 = TT[:, :, 0:mm1]; t2 = self.TT2[:, :, 0:mm1]
                self.tt('dve', t1, qr, Rr1, ALU.mult, ['QR', prk], ['TT'])
                self.tt('dve', t2, qi, Ri1, ALU.mult, ['QI', prk], ['TT2'])
                self.tt('dve', hp[:, 0, :, ca + 1:cb], t1, t2, ALU.subtract, ['TT', 'TT2'], [hpk])
                self.tt('dve', t1, qi, Rr1, ALU.mult, ['QI', prk], ['TT'])
                self.tt('dve', t2, qr, Ri1, ALU.mult, ['QR', prk], ['TT2'])
                self.tt('dve', hp[:, 1, :, ca + 1:cb], t1, t2, ALU.add, ['TT', 'TT2'], [hpk])
        if prepass or self.ssm_lim <= 2:
            return None
        return dict(n=n, nch=nch, KTv=KTv, W3v=W3v, uv=uv, pkk=pkk, ukey=ukey, hp=hp, hpk=hpk)

    def ssm_part2(self, c):
        n = c['n']; nch = c['nch']; KTv = c['KTv']; W3v = c['W3v']; uv = c['uv']
        pkk = c['pkk']; ukey = c['ukey']; hp = c['hp']; hpk = c['hpk']
        pi3, pY = self.ps_get()
        yv = pY[:, 0:n].rearrange("p (c s) -> p s c", s=8)
        first = True
        for sp in range(8):
            for s in range(sp + 1):
                self.mm(yv[:, sp, :], KTv[:, sp - s, :], uv[:, s, :], first, False,
                        [pkk, ukey], [('ps', pi3)], skip_group_check=True)
                first = False
        for sp in range(8 if self.ssm_lim >= 4 else 0):
            for ri in range(2):
                for r in range(4):
                    last = (r == 3 and sp == 7 and ri == 1)
                    self.mm(yv[32 * r:32 * r + 32, sp, :], W3v[:, r, ri, sp, :], hp[:, ri, r, 0:nch], False, last,
                            [pkk, hpk], [('ps', pi3)], tile_position=(0, 32 * r), skip_group_check=True)
        return (pi3, pY)

    def gelu_to(self, out, pY, pkey, n, okey):
        pt = pY
        if self.gelu_native:
            self.actf(out, pt[:, 0:n], AF.Gelu_apprx_tanh, [pkey, 'gateM'], [okey])
            return
        a, ak = self.rg('TF')
        self.actf(a[:, 0:n], pt[:, 0:n], AF.Square, [pkey], [ak])
        self.ts('dve', a[:, 0:n], a[:, 0:n], 0.044715, 1.0, ALU.mult, ALU.add, [ak], [ak])
        self.tt('dve', a[:, 0:n], pt[:, 0:n], a[:, 0:n], ALU.mult, [pkey, ak], [ak])
        b, bk = self.rg('TF')
        self.actf(b[:, 0:n], a[:, 0:n], AF.Sigmoid, [ak], [bk], scale=1.5957691216057308)
        self.tt('dve', out, pt[:, 0:n], b[:, 0:n], ALU.mult, [pkey, bk, 'gateM'], [okey])

    def mkeys(self):
        return ([('OUTA', j) for j in range(8)] + [('YB', j) for j in range(8)] +
                [('OUTB', j) for j in range(8)] + [('MB', j) for j in range(16)])

    def mixer(self, i):
        n = TILE_COLS[i]
        segs = tile_segs(i)
        self.add('dve', lambda g: g.memset(self.gdum[:], 0.0), [], [('ACTB', j) for j in range(FT)] + ['gateM', 'W1ALL'])
        prm = self.prm
        hn = self.hn_rhs(n)
        last_tile = (i == 4)
        if self.lim <= 0:
            return
        for j in range(8):
            pc = self.gemm('c%d' % j, 16, hn, n)
            ph = self.gemm('h%d' % j, 16, hn, n)
            pb = self.gemm('b%d' % j, 16, hn, n)
            tf, tk = self.rg('TF')
            self.cp('act', tf[:, 0:n], pc[1][:, 0:n], [('ps', pc[0])], [tk])
            ch, chk = self.rg('CH')
            t, tk2 = self.rg('TF')
            for si, (c0, c1, st) in enumerate(segs):
                m = c1 - c0
                o = c0 + 2 * (si + 1)
                hk = ('CHH', st, j)
                self.cp('dve', ch[:, o - 2:o], self.CHH[st][:, j, :], [hk], [chk])
                self.tt('dve', ch[:, o:o + m], ph[1][:, c0:c1], tf[:, c0:c1], ALU.mult,
                        [('ps', ph[0]), tk], [chk])
                self.cp('dve', self.CHH[st][:, j, :], ch[:, o + m - 2:o + m], [chk], [hk])
                if last_tile:
                    self.tt('dve', self.CHO[st][:, j, :], ph[1][:, c1 - 2:c1], tf[:, c1 - 2:c1], ALU.mult,
                            [('ps', ph[0]), tk], ['STO'])
                cw = lambda q: prm[:, 48 + 3 * j + q:49 + 3 * j + q]
                self.actf(t[:, c0:c1], ch[:, o - 2:o - 2 + m], AF.Identity, [chk, 'prm'], [tk2], scale=cw(0))
                self.stt(t[:, c0:c1], ch[:, o - 1:o - 1 + m], cw(1), t[:, c0:c1], ALU.mult, ALU.add, [chk, tk2], [tk2])
                self.stt(t[:, c0:c1], ch[:, o:o + m], cw(2), t[:, c0:c1], ALU.mult, ALU.add, [chk, tk2], [tk2])
            self.tt('dve', self.OUTA[:, j, 0:n], pb[1][:, 0:n], t[:, 0:n], ALU.mult, [('ps', pb[0]), tk2, 'gateM'], [('OUTA', j)])
        if self.lim <= 1:
            return
        self.ssm_begin(segs, False)
        def part2(j, ctx):
            pi3, pY = self.ssm_part2(ctx)
            self.gelu_to(self.YB[:, j, 0:n], pY, ('ps', pi3), n, ('YB', j))
            if self.dbg and i == self.dbg_tile:
                self.dbg_dump_ps(j, pY, ('ps', pi3), n, None, None)
        prev = None
        for j in range(8):
            pu = self.gemm('u%d' % j, 16, hn, n)
            u, uk = self.rg('U')
            self.cp('act', u[:, 0:n], pu[1][:, 0:n], [('ps', pu[0])], [uk])
            ctx = self.ssm_tile(j, u, uk, n, segs, False)
            if prev is not None:
                part2(*prev)
            prev = (j, ctx) if ctx is not None else None
        if prev is not None:
            part2(*prev)
        self.ssm_end(segs, False)
        if self.lim <= 2:
            return
        yb = lambda k: (self.YB[:, k, 0:n], ('YB', k))
        for jo in range(8):
            pg = self.gemm('glu%d' % jo, 8, yb, n)
            sg, sk = self.rg('TB')
            self.actf(sg[:, 0:n], pg[1][:, 0:n], AF.Sigmoid, [('ps', pg[0]), 'prm'], [sk], bias=prm[:, 72 + jo:73 + jo])
            self.tt('dve', self.OUTB[:, jo, 0:n], self.YB[:, jo, 0:n], sg[:, 0:n], ALU.mult, [('YB', jo), sk, 'gateM'], [('OUTB', jo)])
        if self.lim <= 3:
            return
        oa = lambda k: (self.OUTA[:, k, 0:n], ('OUTA', k))
        ob = lambda k: (self.OUTB[:, k, 0:n], ('OUTB', k))
        for jo in range(16):
            pa = self.gemm('pa%d' % jo, 8, oa, n)
            pb = self.gemm('pb%d' % jo, 8, ob, n)
            ga = self.gemm('ga%d' % jo, 16, hn, n)
            gb = self.gemm('gb%d' % jo, 16, hn, n)
            sa, sak = self.rg('TF')
            sb_, sbk = self.rg('TF')
            self.actf(sa[:, 0:n], ga[1][:, 0:n], AF.Sigmoid, [('ps', ga[0])], [sak])
            self.actf(sb_[:, 0:n], gb[1][:, 0:n], AF.Sigmoid, [('ps', gb[0])], [sbk])
            self.tt('dve', sa[:, 0:n], pa[1][:, 0:n], sa[:, 0:n], ALU.mult, [('ps', pa[0]), sak], [sak])
            self.tt('dve', sb_[:, 0:n], pb[1][:, 0:n], sb_[:, 0:n], ALU.mult, [('ps', pb[0]), sbk], [sbk])
            self.tt('dve', self.MB[:, jo, 0:n], sa[:, 0:n], sb_[:, 0:n], ALU.add, [sak, sbk, 'gateM'], [('MB', jo)])
        mb = lambda k: (self.MB[:, k, 0:n], ('MB', k))
        for jo in range(16):
            po = self.gemm('wo%d' % jo, 16, mb, n)
            self.tt('dve', self.X[:, jo, 0:n], po[1][:, 0:n], self.X[:, jo, 0:n], ALU.add,
                    [('ps', po[0]), self.xk(jo)], [self.xk(jo)])

    def ffn(self, i):
        n = TILE_COLS[i]
        segs = tile_segs(i)
        self.add('dve', lambda g: g.memset(self.gdum[:], 0.0), [], self.mkeys() + ['gateF'])
        prm = self.prm
        hn = self.hn_rhs(n)
        for j in range(FT):
            res = []
            for (nm, f) in (('ug%d' % j, j), ('uv%d' % j, FT + j)):
                pu = self.gemm(nm, 16, hn, n)
                pk_ = ('ps', pu[0])
                raw, rk = self.rg('RAW')
                t, tk = self.rg('TF')
                fw = lambda q: prm[:, 80 + 3 * f + q:81 + 3 * f + q]
                fb = prm[:, 344 + f:345 + f]
                for si, (c0, c1, st) in enumerate(segs):
                    m = c1 - c0
                    o = c0 + 2 * (si + 1)
                    hk = ('UPH', st, f)
                    self.cp('act', raw[:, o:o + m], pu[1][:, c0:c1], [pk_], [rk])
                    self.cp('dve', raw[:, o - 2:o], self.UPH[st][:, f, :], [hk], [rk])
                    self.actf(t[:, c0:c1], raw[:, o - 2:o - 2 + m], AF.Identity, [rk, 'prm'], [tk], scale=fw(0), bias=fb)
                    self.stt(t[:, c0:c1], raw[:, o - 1:o - 1 + m], fw(1), t[:, c0:c1], ALU.mult, ALU.add, [rk, tk], [tk])
                    self.stt(t[:, c0:c1], pu[1][:, c0:c1], fw(2), t[:, c0:c1], ALU.mult, ALU.add, [pk_, tk], [tk])
                    self.cp('dve', self.UPH[st][:, f, :], raw[:, o + m - 2:o + m], [rk], [hk, 'STO'])
                res.append((t, tk))
            (tg, tgk), (tv, tvk) = res
            self.actf(tg[:, 0:n], tg[:, 0:n], AF.Silu, [tgk], [tgk])
            self.tt('dve', self.ACTB[:, j, 0:n], tg[:, 0:n], tv[:, 0:n], ALU.mult, [tgk, tvk, 'gateF'], [('ACTB', j)])
    def ffn_down(self, i):
        n = TILE_COLS[i]
        ab = lambda k: (self.ACTB[:, k, 0:n], ('ACTB', k))
        for jo in range(16):
            pd = self.gemm('wd%d' % jo, FT, ab, n)
            self.tt('dve', self.X[:, jo, 0:n], pd[1][:, 0:n], self.X[:, jo, 0:n], ALU.add,
                    [('ps', pd[0]), self.xk(jo)], [self.xk(jo)])

    def final(self, i):
        n = TILE_COLS[i]; off = TILE_OFF[i]
        yv = self.d['yT'].rearrange("(k p) n -> p k n", p=128)

        def f(k):
            yo, yk = self.rg('YO')
            self.stt(yo[:, 0:n], self.X[:, k, 0:n], self.prm[:, 32 + k:33 + k], self.RSTD[:, 0:n],
                     ALU.mult, ALU.mult, [self.xk(k), self.rkey, 'prm'], [yk])
            self.stores.append(self.dma('act', 'st', yv[:, k, off:off + n], yo[:, 0:n], [yk], []))
        self.norm(n, f)

    def prepass_bufs(self, i):
        n = 432
        if i % 2 == 0:
            hv = lambda k: self.HN[:, k, 0:n]
            hk = lambda k: ('HN', k)
        else:
            pkb = self.rings_sb['PK']
            hv = lambda k: pkb[k // 8][:, (k % 8) * 432:(k % 8) * 432 + n]
            hk = lambda k: ('PK', k // 8)
        return hv, hk

    def prepass_head(self, i):
        n = 432
        self.set_x(i + NPT)
        self.RSTD = self.RSTDB[i % 2]
        self.rkey = 'RSTD%d' % (i % 2)
        hv, hk = self.prepass_bufs(i)
        self.norm_stats(n)
        for k in range(16):
            self.actf(hv(k), self.X[:, k, 0:n], AF.Identity, [self.xk(k), 'prm'], [hk(k)],
                      scale=self.prm[:, k:k + 1])

    def prepass_tile(self, i):
        n = 432
        segs = [(0, n, 'p')]
        hv, hk = self.prepass_bufs(i)
        rstd = self.RSTDB[i % 2]
        rkey = 'RSTD%d' % (i % 2)
        hn = lambda k: (hv(k), hk(k))

        def uevac(ii, j):
            hv2, hk2 = self.prepass_bufs(ii)
            pu = self.gemm('u%d' % j, 16, lambda k: (hv2(k), hk2(k)), n)
            u, uk = self.rg('U')
            self.tt('dve', u[:, 0:n], pu[1][:, 0:n], self.RSTDB[ii % 2][:, 0:n], ALU.mult,
                    [('ps', pu[0]), 'RSTD%d' % (ii % 2)], [uk])
            return u, uk
        self.ssm_begin(segs, True)
        for j in range(8):
            if j == 0 and self.pre_u0 is not None:
                u, uk = self.pre_u0
                self.pre_u0 = None
            else:
                u, uk = uevac(i, j)
            self.ssm_tile(j, u, uk, n, segs, True)
            if j == 2 and i + 1 < NPT:
                self.prepass_head(i + 1)
            if j == 6 and i + 3 < NPT:
                self.set_x(i + 3 + NPT)
                self.load_x(i + 3, True)
        if i + 1 < NPT:
            self.pre_u0 = uevac(i + 1, 0)
        self.ssm_end(segs, True)

    def dbg_dump_ps(self, j, pY, pkey, n, u, uk):
        t, tk = self.rg('TF')
        self.cp('dve', t[:, 0:n], pY[:, 0:n], [pkey], [tk])
        self.stores.append(self.dma('act', 'st', self.d['dbg'][:, j * 432:j * 432 + n], t[:, 0:n], [tk], []))

    def build(self, ntiles=5, do_prepass=True, do_ffn=True, gelu_native=False, dbg_tile=0, lim=99, ssm_lim=99):
        nc = self.nc
        S = self.S
        self.gelu_native = gelu_native
        self.lim = lim
        self.ssm_lim = ssm_lim
        self.dbg_tile = dbg_tile
        self.declare()
        d = self.d
        es = self.es
        sbp = lambda n, s, t: self.sb(es, n, s, t)
        S.ring('w', NB); S.ring('x', 4); S.ring('pk', 4); S.ring('cv', 8); S.ring('st', 6)
        S.ring('misc', 8)
        self.prm = sbp('prm_s', [128, NPRM], F32)
        self.ctab = sbp('ctab', [128, 11 * 32], F32)
        self.ONES = sbp('ones', [128, 128], BF16)
        self.epsc = sbp('epsc', [128, 1], F32)
        self.STO = sbp('STO', [128, 512], F32)
        self.smp = sbp('smp_s', [128, 256], F32)
        self.CHHt = sbp('CHH', [128, 2, 8, 2], BF16)
        self.TS = sbp('TS', [128, 2, 32], F32)
        self.CIt = sbp('CI', [128, 2, 2, 32], F32)
        self.QLt = sbp('QL', [128, 2, 2, 32], F32)
        self.psg = [es.enter_context(nc.psum_tensor('psg%d' % i, [128, 2048], F32)) for i in range(2)]
        self.ps = [self.psg[i // 4][:, (i % 4) * 512:(i % 4 + 1) * 512] for i in range(8)]
        self.ACTB = sbp('ACTB', [128, FT, NTMAX], BF16)
        self.W1ALL = self.ACTB[:].rearrange("p f n -> p (f n)")[:, 0:8 * 2048].rearrange("p (j e) -> p j e", j=8)
        self.psn = 0
        self.stores = []
        STO = self.STO
        self.CHO = {'p': STO[:, 0:16].rearrange("p (j k) -> p j k", k=2),
                    's': STO[:, 16:32].rearrange("p (j k) -> p j k", k=2)}
        self.UPH = {'p': STO[:, 32:208].rearrange("p (f k) -> p f k", k=2),
                    's': STO[:, 208:384].rearrange("p (f k) -> p f k", k=2)}
        self.HIN = {'p': STO[:, 384:448].rearrange("p (a c) -> p a c", a=2),
                    's': STO[:, 448:512].rearrange("p (a c) -> p a c", a=2)}
        self.CHH = {'p': self.CHHt[:, 0, :, :], 's': self.CHHt[:, 1, :, :]}
        self.CI = [self.CIt[:, 0, :, :], self.CIt[:, 1, :, :]]
        self.QL = [self.QLt[:, 0, :, :], self.QLt[:, 1, :, :]]
        self.dma('sp', 'misc', self.prm[:], d['prm'], [], ['prm'])
        self.dma('sp', 'misc', self.smp[:], d['smp'], [], ['smp'])
        self.add('dve', lambda g: g.memset(self.ONES[:], 1.0), [], ['ONES'])
        self.add('dve', lambda g: g.memset(self.epsc[:], EPS), [], ['epsc'])
        self.add('dve', lambda g: g.memset(STO[:], 0.0), [], ['STO', ('HIN', 'p'), ('HIN', 's')])
        self.add('dve', lambda g: g.memset(self.CHHt[:], 0.0), [], ['CHHt'])
        self.convert_weights()
        self.prologue()
        self.XB = [sbp('X%d' % i, [128, 16, NTMAX], F32) for i in range(2)]
        self.xi = 0
        self.X = self.XB[0]
        self.HN = sbp('HN', [128, 16, NTMAX], BF16)
        self.WR = [sbp('WR%d' % i, [128, SLAB], BF16) for i in range(NB)]
        self.OUTA = self.ACTB[:, 0:8, :]
        self.YB = self.ACTB[:, 8:16, :]
        self.OUTB = self.ACTB[:, 16:24, :]
        self.MB = self.ACTB[:, 24:40, :]
        self.gdum = sbp('gdum', [128, 1], F32)
        self.RSTDB = [sbp('RSTD%d' % i, [128, NTMAX], F32) for i in range(2)]
        self.RSTD = self.RSTDB[0]
        self.rkey = 'RSTD0'
        self.XR = sbp('XR', [128, 4, NCHMAX], F32)
        self.XI = sbp('XI', [128, 4, NCHMAX], F32)
        self.QR = sbp('QR', [128, 4, NCHMAX], F32)
        self.QI = sbp('QI', [128, 4, NCHMAX], F32)
        self.TT = sbp('TT', [128, 4, NCHMAX], F32)
        self.TT2 = sbp('TT2', [128, 4, NCHMAX], F32)
        self.rings_sb = {
            'SQ': [sbp('SQ%d' % i, [128, NTMAX], BF16) for i in range(2)],
            'TF': [sbp('TF%d' % i, [128, NTMAX], F32) for i in range(5)],
            'TB': [sbp('TB%d' % i, [128, NTMAX], BF16) for i in range(2)],
            'RAW': [sbp('RAW%d' % i, [128, NTMAX + 4], F32) for i in range(2)],
            'YO': [sbp('YO%d' % i, [128, NTMAX], F32) for i in range(2)],
            'CH': [sbp('CHb%d' % i, [128, NTMAX + 4], BF16) for i in range(2)],
            'U': [sbp('U%d' % i, [128, NTMAX], BF16) for i in range(3)],
            'PK': [sbp('PK%d' % i, [128, PACK], BF16) for i in range(2)],
            'PR': [sbp('PR%d' % i, [128, 3, 4, NCHMAX], F32) for i in range(2)],
            'HP': [sbp('HP%d' % i, [128, 2, 4, NCHMAX], BF16) for i in range(2)],
        }
        self.ring_n = {}
        self.win = {}
        self.wn = 0
        self.wptr = 0
        pre_slabs = sorted(set(IPOS[x][0] for x in PRE_ITEMS))
        self.wseq = []
        if do_prepass:
            self.wseq += pre_slabs
        last_slab = NSLAB if do_ffn else IPOS['ug0'][0]
        for i in range(ntiles):
            self.wseq += list(range(last_slab))
        smp = self.smp
        self.x_preloaded = False
        self.pre_u0 = None
        if do_prepass:
            self.set_x(NPT)
            self.load_x(0, True)
            self.set_x(NPT + 1)
            self.load_x(1, True)
            self.prepass_head(0)
            self.set_x(NPT + 2)
            self.load_x(2, True)
            for i in range(NPT):
                self.prepass_tile(i)
            self.set_x(0)
            self.RSTD = self.RSTDB[0]
            self.rkey = 'RSTD0'
        self.cp('dve', self.CHH['s'], smp[:, 0:16].rearrange("p (j k) -> p j k", k=2), ['smp', 'CHHt'],
                [('CHH', 's', j) for j in range(8)] + ['CHHt'])
        self.cp('dve', self.UPH['s'], smp[:, 16:192].rearrange("p (f k) -> p f k", k=2), ['smp', 'STO'],
                [('UPH', 's', f) for f in range(2 * FT)])
        self.cp('dve', self.HIN['s'], smp[:, 192:256].rearrange("p (a c) -> p a c", a=2), ['smp', 'STO'],
                [('HIN', 's')])
        for i in range(ntiles):
            self.set_x(i)
            n = TILE_COLS[i]
            if i == 0:
                self.load_x(0)
                self.norm_to_hn(n, 0)
            self.mixer(i)
            if do_ffn:
                self.norm_to_hn(n, 16)
                if i + 1 < ntiles:
                    self.set_x(i + 1)
                    self.load_x(i + 1)
                    self.set_x(i)
                self.ffn(i)
                if i + 1 < ntiles:
                    self.set_x(i + 1)
                    self.norm_to_hn(TILE_COLS[i + 1], 0)
                    self.set_x(i)
                self.ffn_down(i)
            elif i + 1 < ntiles:
                self.set_x(i + 1)
                self.load_x(i + 1)
                self.norm_to_hn(TILE_COLS[i + 1], 0)
                self.set_x(i)
            self.final(i)
        all_sto_keys = ['STO', ('HIN', 'p'), ('HIN', 's')] + [('UPH', st, f) for st in 'ps' for f in range(2 * FT)]
        self.stores.append(self.dma('act', 'st', d['sto'], STO[:], all_sto_keys, []))
        self.add('act', lambda g: g.activation(out=self.epsc[:], in_=self.epsc[:], func=AF.Copy), [], ['epsc'],
                 extra=self.stores)
        engsem = {e: es.enter_context(nc.semaphore('sem_' + e)) for e in Sched.ENG}
        dmasem = {}
        for name, rgd in S.rings.items():
            for i in range(rgd['k']):
                dmasem[(name, i)] = es.enter_context(nc.semaphore('dsem_%s%d' % (name, i)))
        block = es.enter_context(nc.Block())
        S.emit(nc, block, engsem, dmasem)
        es.close()
        return nc


def _fm(v, nt):
    return np.ascontiguousarray(np.asarray(v, np.float32).reshape(nt, 128).T)


def prep_shared(inp):
    f32 = np.float32
    sh = {}
    for nm in ('w_in', 'glu_w', 'proj_a', 'proj_b', 'w_out', 'w_up', 'w_down'):
        sh[nm] = np.ascontiguousarray(np.asarray(inp[nm], f32)[0])
    prm = np.zeros((128, NPRM), f32)
    prm[:, 0:16] = _fm(inp['norm_mix_g'][0], 16)
    prm[:, 16:32] = _fm(inp['norm_ffn_g'][0], 16)
    prm[:, 32:48] = _fm(inp['norm_final_g'], 16)
    cw = np.asarray(inp['conv_a_w'], f32)[0]
    prm[:, 48:72] = cw.reshape(3, 8, 128).transpose(2, 1, 0).reshape(128, 24)
    prm[:, 72:80] = _fm(inp['glu_b'][0], 8)
    fw = np.asarray(inp['ffn_conv_w'], f32)[0]
    prm[:, 80:344] = fw.reshape(3, 88, 128).transpose(2, 1, 0).reshape(128, 264)
    prm[:, 344:432] = _fm(inp['ffn_conv_b'][0], 88)
    prm[:, 432:440] = _fm(np.asarray(inp['ssm_d'], f32)[0].reshape(-1), 8)
    sh['prm'] = prm
    lre = np.asarray(inp['ssm_lambda_re'], f32)[0]
    lim = np.asarray(inp['ssm_lambda_im'], f32)[0]
    ldt = np.broadcast_to(np.asarray(inp['ssm_log_dt'], f32)[0][:, None], (64, 64))
    toA = lambda a: np.ascontiguousarray(a.reshape(32, 2, 64).transpose(1, 2, 0).reshape(128, 32))
    sh['lamA'] = np.ascontiguousarray(np.stack([toA(lre), toA(lim), toA(ldt)], axis=1))
    bre = np.asarray(inp['ssm_b_re'], f32)[0]; bim = np.asarray(inp['ssm_b_im'], f32)[0]
    cre = np.asarray(inp['ssm_c_re'], f32)[0]; cim = np.asarray(inp['ssm_c_im'], f32)[0]

    def layA(x_gph):
        out = np.zeros((2, 64, 32, 2, 16), f32)
        xg = x_gph.reshape(32, 2, 64, 16)
        for g2 in range(2):
            out[g2, :, :, g2, :] = xg[:, g2].transpose(1, 0, 2)
        return out.reshape(128, 1024)
    sh['BA'] = np.ascontiguousarray(np.stack([layA(bre), layA(bim)], axis=1))
    sh['CA'] = np.ascontiguousarray(np.stack([layA(cre.transpose(0, 2, 1)), layA(cim.transpose(0, 2, 1))], axis=1))

    cst = np.zeros((128, 256), f32)
    blk = np.arange(128) // 16
    cst[:, 0:128] = (blk[:, None] == blk[None, :]).astype(f32)
    cst[:, 128:256] = np.eye(128, dtype=f32)
    sh['cst'] = cst
    return sh


def prep_core(inp, c):
    f32 = np.float32
    b, s = c // 4, c % 4
    seq = np.concatenate([np.asarray(inp['meta_tokens'], f32), np.asarray(inp['x_prompt'], f32)[b]], axis=0)
    xt = np.zeros((NTOK, D), f32)
    q0 = SEGLEN * s - 8 - 16
    lo = max(0, -q0)
    xt[lo:NPR] = seq[q0 + lo:q0 + NPR]
    xt[NPR:NTOK] = np.asarray(inp['x_sample'], f32)[c]
    m = {'xT': np.ascontiguousarray(xt.T)}
    xp = np.zeros((NPRE, D), f32)
    p0 = q0 - NPRE
    lo2 = max(0, -p0)
    if lo2 < NPRE:
        xp[lo2:NPRE] = seq[p0 + lo2:p0 + NPRE]
    m['xpre'] = np.ascontiguousarray(xp.T)
    smp = np.zeros((128, 256), f32)
    ca = np.asarray(inp['cache_conv_a'], f32)[0, c]
    smp[:, 0:16] = ca.reshape(2, 8, 128).transpose(2, 1, 0).reshape(128, 16)
    cf = np.asarray(inp['cache_ffn_conv'], f32)[0, c]
    smp[:, 16:192] = cf.reshape(2, 88, 128).transpose(2, 1, 0).reshape(128, 176)
    toA = lambda a: a.reshape(32, 2, 64).transpose(1, 2, 0).reshape(128, 32)
    smp[:, 192:224] = toA(np.asarray(inp['state_ssm_re'], f32)[0, c])
    smp[:, 224:256] = toA(np.asarray(inp['state_ssm_im'], f32)[0, c])
    m['smp'] = smp
    return m


def assemble(res):
    f32 = np.float32
    y_prompt = np.zeros((2, 8192, D), f32)
    y_sample = np.zeros((8, 64, D), f32)
    nca_p = np.zeros((1, 2, 2, 1024), f32); nre_p = np.zeros((1, 2, 64, 64), f32)
    nim_p = np.zeros((1, 2, 64, 64), f32); nff_p = np.zeros((1, 2, 2, 2 * DFF), f32)
    nca_s = np.zeros((1, 8, 2, 1024), f32); nre_s = np.zeros((1, 8, 64, 64), f32)
    nim_s = np.zeros((1, 8, 64, 64), f32); nff_s = np.zeros((1, 8, 2, 2 * DFF), f32)
    fromA = lambda a: a.reshape(2, 64, 32).transpose(2, 0, 1).reshape(64, 64)
    for c in range(8):
        b, s = c // 4, c % 4
        yT = np.asarray(res[c]['yT'])
        sto = np.asarray(res[c]['sto'])
        t0 = SEGLEN * s - 32
        lo = max(0, -t0)
        y_prompt[b, t0 + lo:t0 + SEGLEN] = yT[:, 8 + lo:8 + SEGLEN].T
        y_sample[c] = yT[:, NPR:NTOK].T
        conv = lambda o: sto[:, o:o + 16].reshape(128, 8, 2).transpose(2, 1, 0).reshape(2, 1024)
        ffn = lambda o: sto[:, o:o + 176].reshape(128, 88, 2).transpose(2, 1, 0).reshape(2, 2 * DFF)
        if s == 3:
            nca_p[0, b] = conv(0); nff_p[0, b] = ffn(32)
            nre_p[0, b] = fromA(sto[:, 384:416]); nim_p[0, b] = fromA(sto[:, 416:448])
        nca_s[0, c] = conv(16); nff_s[0, c] = ffn(208)
        nre_s[0, c] = fromA(sto[:, 448:480]); nim_s[0, c] = fromA(sto[:, 480:512])
    return (y_prompt, y_sample, nca_p, nre_p, nim_p, nff_p, nca_s, nre_s, nim_s, nff_s)


def kernel(**inputs):
    sh = prep_shared(inputs)
    in_maps = []
    for c in range(8):
        m = dict(sh)
        m.update(prep_core(inputs, c))
        in_maps.append(m)
    nc = Builder().build()
    res = run_bass_kernel_spmd(nc, in_maps, core_ids=list(range(8)))
    return assemble(res.results)
```

```python
import numpy as np
from contextlib import ExitStack
import concourse.bass as bass
import concourse.mybir as mybir
from concourse.bass_utils import run_bass_kernel_spmd

F32 = mybir.dt.float32
BF16 = mybir.dt.bfloat16
ALU = mybir.AluOpType
AF = mybir.ActivationFunctionType

D = 2048
KT = 16
DFF = 5632
FT = 44
TILE_COLS = [432, 432, 432, 432, 400]
TILE_OFF = [0, 432, 864, 1296, 1728]
NTOK = 2128
NPR = 2064
SEGLEN = 2056
NTMAX = 432
NCHMAX = 54
EPS = 1e-6
SLAB = 6144
NB = 3
PACK = 5120
NPRM = 440
NPT = 15
NPRE = NPT * 432
SAFE_SAME_ENGINE = True


def tile_segs(i, prepass=False):
    n = TILE_COLS[i]
    if i < 4:
        return [(0, n, 'p')]
    if prepass:
        return [(0, 328, 'p')]
    return [(0, 336, 'p'), (336, 400, 's')]


class Op:
    __slots__ = ('eng', 'fn', 'deps', 'dma', 'semi', 'val', 'idx', 'marked', 'count')


class Sched:
    ENG = ['pe', 'act', 'dve', 'pool', 'sp']

    def __init__(self):
        self.ops = {e: [] for e in self.ENG}
        self.lw = {}
        self.rd = {}
        self.fence = []
        self.rings = {}

    def ring(self, name, k, inc=16):
        self.rings[name] = {'k': k, 'n': 0, 'last': [None] * k, 'inc': inc}

    def add(self, eng, fn, reads=(), writes=(), dma=None, extra=()):
        op = Op()
        op.eng = eng; op.fn = fn; op.dma = dma; op.marked = False; op.count = 0
        op.semi = None; op.val = 0
        deps = list(extra) + list(self.fence)
        for k in reads:
            w = self.lw.get(k)
            if w is not None:
                deps.append(w)
        for k in writes:
            w = self.lw.get(k)
            if w is not None:
                deps.append(w)
            r = self.rd.get(k)
            if r:
                deps.extend(r[0].values())
                deps.extend(r[1])
        if dma is not None:
            rg = self.rings[dma]
            i = rg['n'] % rg['k']
            op.semi = (dma, i)
            op.val = rg['inc'] * (rg['n'] // rg['k'] + 1)
            if rg['last'][i] is not None:
                deps.append(rg['last'][i])
            rg['last'][i] = op
            rg['n'] += 1
        by_eng = {}
        by_dma = {}
        for d in deps:
            if d is op:
                continue
            if d.dma is not None:
                o = by_dma.get(d.semi)
                if o is None or o.val < d.val:
                    by_dma[d.semi] = d
            else:
                o = by_eng.get(d.eng)
                if o is None or o.idx < d.idx:
                    by_eng[d.eng] = d
        op.deps = list(by_eng.values()) + list(by_dma.values())
        op.idx = len(self.ops[eng])
        self.ops[eng].append(op)
        for k in writes:
            self.lw[k] = op
            self.rd[k] = [{}, []]
        for k in reads:
            r = self.rd.get(k)
            if r is None:
                r = self.rd[k] = [{}, []]
            if dma is not None:
                r[1].append(op)
            else:
                r[0][eng] = op
        return op

    def set_fence(self):
        f = []
        for e in ('act', 'dve', 'pe'):
            if self.ops[e]:
                f.append(self.ops[e][-1])
        for e in self.ENG:
            for o in self.ops[e]:
                if o.dma is not None and o.dma != 'cv':
                    f.append(o)
        self.fence = f

    def finalize(self):
        for e in self.ENG:
            for op in self.ops[e]:
                for d in op.deps:
                    if d.dma is None and (d.eng != op.eng or (SAFE_SAME_ENGINE and d.eng != 'pe')):
                        d.marked = True
        for e in self.ENG:
            c = 0
            for op in self.ops[e]:
                if op.marked:
                    c += 1
                op.count = c

    def emit(self, nc, block, engsem, dmasem):
        self.finalize()
        sched = self

        def run(engname, eng):
            waited = {}
            for op in sched.ops[engname]:
                for d in op.deps:
                    if d.dma is not None:
                        key = ('dma',) + d.semi
                        val = d.val
                        sem = dmasem[d.semi]
                    else:
                        if d.eng == engname and not (SAFE_SAME_ENGINE and engname != 'pe'):
                            continue
                        key = ('eng', d.eng)
                        val = d.count
                        sem = engsem[d.eng]
                    if waited.get(key, 0) < val:
                        eng.wait_ge(sem, val)
                        waited[key] = val
                ins = op.fn(eng)
                if op.marked:
                    ins.then_inc(engsem[engname], 1)
                if op.dma is not None:
                    ins.then_inc(dmasem[op.semi], sched.rings[op.dma]['inc'])

        @block.tensor
        def _(e):
            run('pe', e)

        @block.scalar
        def _(e):
            run('act', e)

        @block.vector
        def _(e):
            run('dve', e)

        @block.gpsimd
        def _(e):
            run('pool', e)

        @block.sync
        def _(e):
            run('sp', e)


def weight_items():
    it = []
    for j in range(8):
        it.append(('w_in', 16, 1024 + 128 * j, 'c%d' % j))
        it.append(('w_in', 16, 2048 + 128 * j, 'h%d' % j))
        it.append(('w_in', 16, 128 * j, 'b%d' % j))
    for j in range(8):
        it.append(('w_in', 16, 3072 + 128 * j, 'u%d' % j))
    for j in range(8):
        it.append(('glu_w', 8, 128 * j, 'glu%d' % j))
    for j in range(16):
        it.append(('proj_a', 8, 128 * j, 'pa%d' % j))
        it.append(('proj_b', 8, 128 * j, 'pb%d' % j))
        it.append(('w_in', 16, 4096 + 128 * j, 'ga%d' % j))
        it.append(('w_in', 16, 6144 + 128 * j, 'gb%d' % j))
    for j in range(16):
        it.append(('w_out', 16, 128 * j, 'wo%d' % j))
    for j in range(FT):
        it.append(('w_up', 16, 128 * j, 'ug%d' % j))
        it.append(('w_up', 16, DFF + 128 * j, 'uv%d' % j))
    for j in range(16):
        it.append(('w_down', FT, 128 * j, 'wd%d' % j))
    slabs = []
    cur = []
    used = 0
    pos = {}
    for x in it:
        sz = x[1] * 128
        if used + sz > SLAB:
            slabs.append((cur, used))
            cur = []
            used = 0
        pos[x[3]] = (len(slabs), used)
        cur.append(x)
        used += sz
    slabs.append((cur, used))
    return it, slabs, pos


ITEMS, SLABS, IPOS = weight_items()
NSLAB = len(SLABS)
PRE_ITEMS = ['u%d' % j for j in range(8)]


class Builder:
    def __init__(self, dbg=False, use_cc=True, ncores=8):
        self.dbg = dbg
        self.use_cc = use_cc
        self.ncores = ncores
        self.nc = bass.Bass("TRN2", target_bir_lowering=False)
        self.S = Sched()
        self.es = ExitStack()
        self.uid = 0

    def din(self, name, shape, dt=F32):
        return self.nc.dram_tensor(name, list(shape), dt, kind="ExternalInput").ap()

    def dout(self, name, shape, dt=F32):
        return self.nc.dram_tensor(name, list(shape), dt, kind="ExternalOutput").ap()

    def dscr(self, name, shape, dt):
        return self.nc.dram_tensor(name, list(shape), dt).ap()

    def sb(self, stack, name, shape, dt):
        return stack.enter_context(self.nc.sbuf_tensor(name, list(shape), dt))

    def add(self, *a, **k):
        return self.S.add(*a, **k)

    def tt(self, e, out, a, b, op, r, w):
        return self.add(e, lambda g: g.tensor_tensor(out=out, in0=a, in1=b, op=op), r, w)

    def ts(self, e, out, a, s1, s2, op0, op1, r, w):
        if op1 is None:
            return self.add(e, lambda g: g.tensor_scalar(out=out, in0=a, scalar1=s1, scalar2=None, op0=op0), r, w)
        return self.add(e, lambda g: g.tensor_scalar(out=out, in0=a, scalar1=s1, scalar2=s2, op0=op0, op1=op1), r, w)

    def stt(self, out, a, s, b, op0, op1, r, w):
        return self.add('dve', lambda g: g.scalar_tensor_tensor(out=out, in0=a, scalar=s, in1=b, op0=op0, op1=op1), r, w)

    def cp(self, e, out, a, r, w):
        if e == 'act':
            return self.add(e, lambda g: g.activation(out=out, in_=a, func=AF.Copy), r, w)
        return self.add(e, lambda g: g.tensor_copy(out=out, in_=a), r, w)

    def actf(self, out, a, func, r, w, scale=1.0, bias=0.0):
        return self.add('act', lambda g: g.activation(out=out, in_=a, func=func, scale=scale, bias=bias), r, w)

    def mm(self, out, lhsT, rhs, start, stop, r, w, **kw):
        return self.add('pe', lambda g: g.matmul(out, lhsT, rhs, start=start, stop=stop, **kw), r, w)

    def dma(self, q, ring, out, in_, r, w, **kw):
        return self.add(q, lambda g: g.dma_start(out=out, in_=in_, **kw), r, w, dma=ring)


    def declare(self):
        d = {}
        d['xT'] = self.din('xT', [D, NTOK])
        for nm, shp in (('w_in', [D, 8192]), ('glu_w', [1024, 1024]), ('proj_a', [1024, D]),
                        ('proj_b', [1024, D]), ('w_out', [D, D]), ('w_up', [D, 2 * DFF]),
                        ('w_down', [DFF, D])):
            d[nm] = self.din(nm, shp)
        d['prm'] = self.din('prm', [128, NPRM])
        d['lamA'] = self.din('lamA', [128, 3, 32])
        d['BA'] = self.din('BA', [128, 2, 1024])
        d['CA'] = self.din('CA', [128, 2, 1024])
        d['cst'] = self.din('cst', [128, 256])
        d['xpre'] = self.din('xpre', [D, NPRE])
        d['smp'] = self.din('smp', [128, 16 + 176 + 64])
        d['yT'] = self.dout('yT', [D, NTOK])
        d['sto'] = self.dout('sto', [128, 512])
        d['wscr'] = self.dscr('wscr', [NSLAB, 128, SLAB], BF16)
        d['ssmw'] = self.dscr('ssmw', [8, 128, PACK], BF16)
        d['ssmr'] = self.dscr('ssmr', [8, 128, 3 * 4 * NCHMAX], F32)
        if self.dbg:
            d['dbg'] = self.dout('dbg', [128, 8192])
        self.d = d

    class Arr:
        __slots__ = ('ap', 'key')

        def __init__(self, ap, key):
            self.ap = ap
            self.key = key

    def palloc(self, P, F):
        a = Builder.Arr(self.pp[0:P, self.ppos:self.ppos + F], ('pp', self.ppos))
        self.ppos += F
        assert self.ppos <= self.ppcols
        return a

    def p_tt(self, o, a, b, op):
        self.tt('dve', o.ap, a.ap, b.ap, op, [a.key, b.key], [o.key])

    def p_ts(self, o, a, m, c):
        self.ts('dve', o.ap, a.ap, float(m), float(c), ALU.mult, ALU.add, [a.key], [o.key])

    def p_cmul(self, o, a, b, t):
        self.p_tt(t[0], a[0], b[0], ALU.mult)
        self.p_tt(t[1], a[1], b[1], ALU.mult)
        self.p_tt(o[0], t[0], t[1], ALU.subtract)
        self.p_tt(t[0], a[0], b[1], ALU.mult)
        self.p_tt(t[1], a[1], b[0], ALU.mult)
        self.p_tt(o[1], t[0], t[1], ALU.add)

    def p_csq(self, o, a, t):
        self.p_tt(t[0], a[0], a[0], ALU.mult)
        self.p_tt(t[1], a[1], a[1], ALU.mult)
        self.p_tt(o[0], t[0], t[1], ALU.subtract)
        self.p_tt(t[0], a[0], a[1], ALU.mult)
        self.p_ts(o[1], t[0], 2.0, 0.0)

    def ptab(self, P, F, lr, li, ldt):
        A = lambda: self.palloc(P, F)
        dt = A()
        self.actf(dt.ap, ldt.ap, AF.Exp, [ldt.key], [dt.key])
        x = A(); th = A()
        self.p_tt(x, lr, dt, ALU.mult)
        self.p_tt(th, li, dt, ALU.mult)
        t = (A(), A())
        mag = A()
        self.p_ts(mag, x, 1.0 / 6.0, 1.0)
        for c in (5, 4, 3, 2, 1):
            self.p_tt(t[0], mag, x, ALU.mult)
            self.p_ts(mag, t[0], 1.0 / c, 1.0)
        y = A(); z = A()
        self.p_ts(y, th, 1.0 / 16.0, 0.0)
        self.p_tt(z, y, y, ALU.mult)
        sn = A(); cs = A()
        self.p_ts(sn, z, -1.0 / 156.0, 1.0)
        for c in (110, 72, 42, 20, 6):
            self.p_tt(t[0], sn, z, ALU.mult)
            self.p_ts(sn, t[0], -1.0 / c, 1.0)
        self.p_tt(t[0], sn, y, ALU.mult)
        self.p_ts(sn, t[0], 1.0, 0.0)
        self.p_ts(cs, z, -1.0 / 132.0, 1.0)
        for c in (90, 56, 30, 12, 2):
            self.p_tt(t[0], cs, z, ALU.mult)
            self.p_ts(cs, t[0], -1.0 / c, 1.0)
        cur = (cs, sn)
        for _ in range(4):
            nxt = (A(), A())
            self.p_csq(nxt, cur, t)
            cur = nxt
        pw = {1: (A(), A())}
        self.p_tt(pw[1][0], mag, cur[0], ALU.mult)
        self.p_tt(pw[1][1], mag, cur[1], ALU.mult)
        for k in range(2, 9):
            pw[k] = (A(), A())
            self.p_cmul(pw[k], pw[k - 1], pw[1], t)
        nr = A(); den = A(); inv = A(); u1 = A()
        self.p_ts(nr, pw[1][0], 1.0, -1.0)
        self.p_tt(t[0], lr, lr, ALU.mult)
        self.p_tt(t[1], li, li, ALU.mult)
        self.p_tt(den, t[0], t[1], ALU.add)
        self.add('dve', lambda g: g.reciprocal(out=inv.ap, in_=den.ap), [den.key], [inv.key])
        cf = (A(), A())
        self.p_tt(t[0], nr, lr, ALU.mult)
        self.p_tt(t[1], pw[1][1], li, ALU.mult)
        self.p_tt(u1, t[0], t[1], ALU.add)
        self.p_tt(cf[0], u1, inv, ALU.mult)
        self.p_tt(t[0], pw[1][1], lr, ALU.mult)
        self.p_tt(t[1], nr, li, ALU.mult)
        self.p_tt(u1, t[0], t[1], ALU.subtract)
        self.p_tt(cf[1], u1, inv, ALU.mult)
        G = {0: cf}
        for k in range(1, 8):
            G[k] = (A(), A())
            self.p_cmul(G[k], pw[k], cf, t)
        return dict(pw=pw, G=G, mag=mag, t=t, A=A)

    def prologue(self):
        d = self.d
        st = ExitStack()
        sb = lambda n, s, t: self.sb(st, n, s, t)
        self.ppcols = 10240
        self.pp = sb('pp', [128, self.ppcols], F32)
        self.ppos = 0
        lamA = sb('lamA_s', [128, 3, 32], F32)
        BA = sb('BA_s', [128, 2, 1024], F32)
        CA = sb('CA_s', [128, 2, 1024], F32)
        NCA = sb('NCA_s', [128, 2, 1024], F32)
        cst = sb('cst_s', [128, 256], F32)
        W1st = self.W1ALL.rearrange("p j (a s c) -> p j a s c", a=2, s=8)
        W3st = sb('W3st', [128, 32, 2, 8, 32], BF16)
        KTst = sb('KTst', [128, 8, 8, 128], BF16)
        RT = sb('RT', [128, 3, 32, NCHMAX], F32)
        T1 = sb('T1', [128, 1024], F32)
        T2 = sb('T2', [128, 1024], F32)
        EA = [sb('EA%d' % i, [128, 2, 1024], F32) for i in range(1)]
        TK = sb('TK', [128, 128], F32)
        prm = self.prm
        for nm, t_, src in (('lamA', lamA, d['lamA']), ('BA', BA, d['BA']),
                            ('CA', CA, d['CA']), ('cst', cst, d['cst'])):
            self.dma('sp', 'misc', t_[:], src, [], [nm])
        la = [Builder.Arr(lamA[:, i, :], 'lamA') for i in range(3)]
        tA = self.ptab(128, 32, la[0], la[1], la[2])
        A = tA['A']; t = tA['t']; pw = tA['pw']
        ct = self.ctab
        ck = lambda i, n=32: Builder.Arr(ct[:, i * 32:i * 32 + n], ('ct', i))
        A8 = (ck(0), ck(1)); RHO = ck(2)
        self.cp('dve', A8[0].ap, pw[8][0].ap, [pw[8][0].key], [A8[0].key])
        self.cp('dve', A8[1].ap, pw[8][1].ap, [pw[8][1].key], [A8[1].key])
        m2 = A(); m4 = A(); irho = A()
        self.p_tt(m2, tA['mag'], tA['mag'], ALU.mult)
        self.p_tt(m4, m2, m2, ALU.mult)
        self.p_tt(RHO, m4, m4, ALU.mult)
        self.add('dve', lambda g: g.reciprocal(out=irho.ap, in_=RHO.ap), [RHO.key], [irho.key])
        U = (A(), A())
        self.p_tt(U[0], A8[0], irho, ALU.mult)
        self.p_tt(U[1], A8[1], irho, ALU.mult)
        self.add('dve', lambda g: g.memset(RT[:, 0, :, 0:1], 1.0), [], ['RT'])
        self.add('dve', lambda g: g.memset(RT[:, 1, :, 0:1], 0.0), [], ['RT'])
        n = 1
        Un = U
        while n < NCHMAX:
            m = min(n, NCHMAX - n)
            bc = lambda a: a.ap.unsqueeze(2).to_broadcast([128, 32, m])
            o_r = RT[:, 0, :, n:n + m]; o_i = RT[:, 1, :, n:n + m]
            i_r = RT[:, 0, :, 0:m]; i_i = RT[:, 1, :, 0:m]
            v1 = T1[:, 0:32 * m].rearrange("p (a c) -> p a c", c=m)
            v2 = T2[:, 0:32 * m].rearrange("p (a c) -> p a c", c=m)
            kk = ['RT', Un[0].key, Un[1].key, 'T1', 'T2']
            self.tt('dve', v1, i_r, bc(Un[0]), ALU.mult, kk, ['T1'])
            self.tt('dve', v2, i_i, bc(Un[1]), ALU.mult, kk, ['T2'])
            self.tt('dve', o_r, v1, v2, ALU.subtract, kk, ['RT'])
            self.tt('dve', v1, i_r, bc(Un[1]), ALU.mult, kk, ['T1'])
            self.tt('dve', v2, i_i, bc(Un[0]), ALU.mult, kk, ['T2'])
            self.tt('dve', o_i, v1, v2, ALU.add, kk, ['RT'])
            n += m
            if n < NCHMAX:
                nx = (A(), A())
                self.p_csq(nx, Un, t)
                Un = nx
        self.add('dve', lambda g: g.memset(RT[:, 2, :, 0:1], 0.0), [], ['RT'])
        self.cp('dve', RT[:, 2, :, 1:NCHMAX], RHO.ap.unsqueeze(2).to_broadcast([128, 32, NCHMAX - 1]), [RHO.key], ['RT'])
        for j in range(8):
            self.dma('sp', 'misc', d['ssmr'][j].rearrange("p (a r c) -> p a r c", a=3, r=4),
                     RT[:, :, 4 * j:4 * j + 4, :], ['RT'], ['ssmr'])
        self.rl = {}
        for idx, m in enumerate((54, 42, 8)):
            a = (ck(3 + 2 * idx), ck(4 + 2 * idx))
            for ri in range(2):
                self.cp('dve', a[ri].ap, RT[:, ri, :, m - 1], ['RT'], [a[ri].key])
            self.rl[m] = a
        self.A8 = A8; self.RHO = RHO
        self.ts('dve', NCA[:], CA[:], -1.0, None, ALU.mult, None, ['CA'], ['NCA'])
        bm = cst[:, 0:128]; ident = cst[:, 128:256]
        for k in range(8):
            e = EA[0]; ek = 'EA0'
            gr, gi = tA['G'][k]
            bc = lambda a: a.ap.unsqueeze(2).to_broadcast([128, 32, 32])
            v = lambda x: x.rearrange("p (a c) -> p a c", c=32)
            kk = [gr.key, gi.key, 'BA', 'T1', 'T2']
            self.tt('dve', v(T1[:]), v(BA[:, 0, :]), bc(gr), ALU.mult, kk, ['T1'])
            self.tt('dve', v(T2[:]), v(BA[:, 1, :]), bc(gi), ALU.mult, kk, ['T2'])
            self.tt('dve', e[:, 0, :], T1[:], T2[:], ALU.subtract, ['T1', 'T2'], [ek])
            self.tt('dve', v(T1[:]), v(BA[:, 0, :]), bc(gi), ALU.mult, kk, ['T1'])
            self.tt('dve', v(T2[:]), v(BA[:, 1, :]), bc(gr), ALU.mult, kk, ['T2'])
            self.tt('dve', e[:, 1, :], T1[:], T2[:], ALU.add, ['T1', 'T2'], [ek])
            for ri in range(2):
                for hf in range(2):
                    pi, ps = self.ps_get()
                    for jj in range(4):
                        j = 4 * hf + jj
                        self.mm(ps[:, 128 * jj:128 * jj + 128], e[:, ri, 128 * j:128 * j + 128], ident, True, True,
                                [ek, 'cst'], [('ps', pi)])
                    self.cp('act', W1st[:, 4 * hf:4 * hf + 4, ri, 7 - k, :],
                            ps[:, 0:512].rearrange("p (a c) -> p a c", c=128), [('ps', pi)], ['W1ALL'])
            for j in range(8):
                pi, ps = self.ps_get()
                sl = slice(128 * j, 128 * j + 128)
                self.mm(ps[:, 0:128], e[:, 0, sl], CA[:, 0, sl], True, False, [ek, 'CA'], [('ps', pi)])
                self.mm(ps[:, 0:128], e[:, 1, sl], NCA[:, 1, sl], False, True, [ek, 'NCA'], [('ps', pi)])
                if k == 0:
                    self.tt('dve', TK[:], ps[:, 0:128], bm, ALU.mult, [('ps', pi), 'cst'], ['TK'])
                    self.stt(KTst[:, j, k, :], ident, prm[:, 432 + j:433 + j], TK[:], ALU.mult, ALU.add,
                             ['TK', 'cst', 'prm'], ['KTst'])
                else:
                    self.tt('dve', KTst[:, j, k, :], ps[:, 0:128], bm, ALU.mult, [('ps', pi), 'cst'], ['KTst'])
        for k in range(1, 9):
            ar, ai = pw[k]
            bc = lambda a: a.ap.unsqueeze(2).to_broadcast([128, 32, 32])
            v = lambda x: x.rearrange("p (a c) -> p a c", c=32)
            kk = [ar.key, ai.key, 'CA', 'NCA', 'T1', 'T2']
            self.tt('dve', v(T1[:]), v(CA[:, 0, :]), bc(ar), ALU.mult, kk, ['T1'])
            self.tt('dve', v(T2[:]), v(CA[:, 1, :]), bc(ai), ALU.mult, kk, ['T2'])
            self.tt('dve', W3st[:, :, 0, k - 1, :], v(T1[:]), v(T2[:]), ALU.subtract, ['T1', 'T2'], ['W3st'])
            self.tt('dve', v(T1[:]), v(NCA[:, 0, :]), bc(ai), ALU.mult, kk, ['T1'])
            self.tt('dve', v(T2[:]), v(CA[:, 1, :]), bc(ar), ALU.mult, kk, ['T2'])
            self.tt('dve', W3st[:, :, 1, k - 1, :], v(T1[:]), v(T2[:]), ALU.subtract, ['T1', 'T2'], ['W3st'])
        sw = d['ssmw'].rearrange("j p e -> p j e")
        self.dma('sp', 'misc', sw[:, :, 0:2048], self.W1ALL, ['W1ALL'], ['ssmw'])
        self.dma('sp', 'misc', sw[:, :, 2048:4096],
                 W3st[:].rearrange("p (j r) a s c -> p j (r a s c)", r=4), ['W3st'], ['ssmw'])
        self.dma('sp', 'misc', sw[:, :, 4096:5120], KTst[:].rearrange("p j k c -> p j (k c)"), ['KTst'], ['ssmw'])
        self.S.set_fence()
        st.close()

    def xk(self, k):
        return ('X', self.xi, k)

    def set_x(self, i):
        self.xi = i % 2
        self.X = self.XB[self.xi]

    def ps_get(self):
        i = self.psn % 8
        self.psn += 1
        return i, self.ps[i]

    def ps_get4(self):
        while self.psn % 4 != 0:
            self.psn += 1
        i = self.psn % 8
        self.psn += 4
        return i, self.psg[i // 4]

    def rg(self, name):
        bufs = self.rings_sb[name]
        i = self.ring_n.get(name, 0)
        self.ring_n[name] = i + 1
        k = i % len(bufs)
        return bufs[k], (name, k)

    def ensure(self, slab):
        if slab in self.win:
            return self.win[slab]
        while True:
            s = self.wseq[self.wptr]
            self.issue_slab()
            if s == slab:
                return self.win[slab]

    def issue_slab(self):
        s = self.wseq[self.wptr]
        self.wptr += 1
        slot = self.wn % NB
        self.wn += 1
        for k in [k for k, v in self.win.items() if v == slot]:
            del self.win[k]
        used = SLABS[s][1]
        self.dma('sp', 'w', self.WR[slot][:, 0:used], self.d['wscr'][s][:, 0:used],
                 [('scr', s, x[3]) for x in SLABS[s][0]], [('wr', slot)])
        self.win[s] = slot

    def prefetch(self, k):
        for _ in range(k):
            if self.wptr < len(self.wseq) and len(self.win) < NB:
                self.issue_slab()

    def gemm(self, name, nk, rhs_fn, n, first=True, last=True, ps=None):
        slab, off = IPOS[name]
        slot = self.ensure(slab)
        if ps is None:
            ps = self.ps_get()
        pi, pt = ps
        for k in range(nk):
            lhsT = self.WR[slot][:, off + k * 128: off + (k + 1) * 128]
            rhs, rkey = rhs_fn(k)
            self.mm(pt[:, 0:n], lhsT, rhs, first and k == 0, last and k == nk - 1,
                    [('wr', slot), rkey], [('ps', pi)])
        return ps

    def convert_weights(self):
        d = self.d
        order = [x for x in ITEMS if x[3] in PRE_ITEMS] + [x for x in ITEMS if x[3] not in PRE_ITEMS]
        for (mat, nk, c0, nm) in order:
            slab, off = IPOS[nm]
            src = d[mat].rearrange("(k p) n -> p k n", p=128)[:, :, c0:c0 + 128]
            dst = d['wscr'][slab][:, off:off + nk * 128].rearrange("p (k c) -> p k c", c=128)
            self.dma('pool', 'cv', dst, src, [], [('scr', slab, nm)])

    def load_x(self, i, pre=False):
        if pre:
            n = 432; off = 432 * i
            xv = self.d['xpre'].rearrange("(k p) n -> p k n", p=128)
        else:
            n = TILE_COLS[i]; off = TILE_OFF[i]
            xv = self.d['xT'].rearrange("(k p) n -> p k n", p=128)
        for q in range(4):
            self.dma('sp', 'x', self.X[:, 4 * q:4 * q + 4, 0:n], xv[:, 4 * q:4 * q + 4, off:off + n],
                     [], [self.xk(k) for k in range(4 * q, 4 * q + 4)])

    def norm_stats(self, n):
        ps = self.ps_get()
        pi, pt = ps
        for k in range(16):
            sq, sk = self.rg('SQ')
            self.actf(sq[:, 0:n], self.X[:, k, 0:n], AF.Square, [self.xk(k)], [sk])
            self.mm(pt[:, 0:n], self.ONES[:], sq[:, 0:n], k == 0, k == 15, [sk, 'ONES'], [('ps', pi)])
        tf, tk = self.rg('TF')
        self.actf(tf[:, 0:n], pt[:, 0:n], AF.Sqrt, [('ps', pi)], [tk], scale=1.0 / D, bias=self.epsc[:, 0:1])
        rs = self.RSTD
        self.add('dve', lambda g: g.reciprocal(out=rs[:, 0:n], in_=tf[:, 0:n]), [tk], [self.rkey])

    def norm(self, n, out_fn):
        self.norm_stats(n)
        for k in range(16):
            out_fn(k)

    def norm_to_hn(self, n, gofs):
        def f(k):
            self.stt(self.HN[:, k, 0:n], self.X[:, k, 0:n], self.prm[:, gofs + k:gofs + k + 1],
                     self.RSTD[:, 0:n], ALU.mult, ALU.mult, [self.xk(k), self.rkey, 'prm'], [('HN', k)])
        self.norm(n, f)

    def hn_rhs(self, n):
        return lambda k: (self.HN[:, k, 0:n], ('HN', k))

    def ssm_begin(self, segs, prepass):
        for si, (c0, c1, st) in enumerate(segs):
            H = self.HIN[st]
            ci = self.CI[si]
            k = ('CI', si)
            hk = ('HIN', st)
            t1 = self.TS[:, 0, :]; t2 = self.TS[:, 1, :]
            A8 = self.A8
            rk = [hk, A8[0].key, A8[1].key, 'TS']
            self.tt('dve', t1, H[:, 0, :], A8[0].ap, ALU.mult, rk, ['TS'])
            self.tt('dve', t2, H[:, 1, :], A8[1].ap, ALU.mult, rk, ['TS'])
            self.tt('dve', ci[:, 0, :], t1, t2, ALU.subtract, ['TS'], [k])
            self.tt('dve', t1, H[:, 0, :], A8[1].ap, ALU.mult, rk, ['TS'])
            self.tt('dve', t2, H[:, 1, :], A8[0].ap, ALU.mult, rk, ['TS'])
            self.tt('dve', ci[:, 1, :], t1, t2, ALU.add, ['TS'], [k])

    def ssm_end(self, segs, prepass):
        for si, (c0, c1, st) in enumerate(segs):
            m = (c1 - c0) // 8
            rl = self.rl[m]
            ql = self.QL[si]
            H = self.HIN[st]
            t1 = self.TS[:, 0, :]; t2 = self.TS[:, 1, :]
            rk = [('QL', si), rl[0].key, rl[1].key, 'TS']
            hk = ('HIN', st)
            self.tt('dve', t1, ql[:, 0, :], rl[0].ap, ALU.mult, rk, ['TS'])
            self.tt('dve', t2, ql[:, 1, :], rl[1].ap, ALU.mult, rk, ['TS'])
            self.tt('dve', H[:, 0, :], t1, t2, ALU.subtract, ['TS'], [hk])
            self.tt('dve', t1, ql[:, 0, :], rl[1].ap, ALU.mult, rk, ['TS'])
            self.tt('dve', t2, ql[:, 1, :], rl[0].ap, ALU.mult, rk, ['TS'])
            self.tt('dve', H[:, 1, :], t1, t2, ALU.add, ['TS'], [hk])

    def ssm_tile(self, j, u, ukey, n, segs, prepass):
        d = self.d
        nch = n // 8
        pr, prk = self.rg('PR')
        if prepass:
            pk = None
            pkk = 'W1ALL'
        else:
            pk, pkk = self.rg('PK')
            self.dma('sp', 'pk', pk[:], d['ssmw'][j], ['ssmw'], [pkk])
        self.dma('sp', 'pk', pr[:], d['ssmr'][j].rearrange("p (a r c) -> p a r c", a=3, r=4), ['ssmr'], [prk])
        if prepass:
            W1v = self.W1ALL[:, j, :].rearrange("p (a s c) -> p a s c", a=2, s=8)
            W3v = KTv = None
        else:
            W1v = pk[:, 0:2048].rearrange("p (a s c) -> p a s c", a=2, s=8)
            W3v = pk[:, 2048:4096].rearrange("p (r a s c) -> p r a s c", r=4, a=2, s=8)
            KTv = pk[:, 4096:5120].rearrange("p (k c) -> p k c", k=8)
        uv = u[:, 0:n].rearrange("p (c s) -> p s c", s=8)
        pi1, pG = self.ps_get4()
        skeys = [('ps', pi1 + r) for r in range(4)]
        for ri in range(2):
            for s in range(8):
                for r in range(4):
                    o = r * 512 + ri * 256
                    self.mm(pG[:, o:o + nch], W1v[32 * r:32 * r + 32, ri, s, :], uv[32 * r:32 * r + 32, s, :],
                            s == 0, s == 7, [pkk, ukey], [skeys[r]], tile_position=(32 * r, 0))
        if self.ssm_lim <= 0:
            return None
        Sv = [pG[:].rearrange("p (r c) -> p r c", r=4)[:, :, ri * 256:ri * 256 + nch] for ri in range(2)]
        hp, hpk = self.rg('HP')
        XR, XI, QR, QI, TT = self.XR, self.XI, self.QR, self.QI, self.TT
        for si, (c0, c1, st) in enumerate(segs):
            ca, cb = c0 // 8, c1 // 8
            m = cb - ca
            Rr = pr[:, 0, :, 0:m]; Ri = pr[:, 1, :, 0:m]
            Sr = Sv[0][:, :, ca:cb]; Si = Sv[1][:, :, ca:cb]
            xr = XR[:, :, 0:m]; xi = XI[:, :, 0:m]; tt_ = TT[:, :, 0:m]
            self.tt('dve', xr, Sr, Rr, ALU.mult, skeys + [prk], ['XR'])
            self.tt('dve', tt_, Si, Ri, ALU.mult, skeys + [prk], ['TT'])
            self.tt('dve', xr, xr, tt_, ALU.add, ['XR', 'TT'], ['XR'])
            self.tt('dve', xi, Si, Rr, ALU.mult, skeys + [prk], ['XI'])
            self.tt('dve', tt_, Sr, Ri, ALU.mult, skeys + [prk], ['TT'])
            self.tt('dve', xi, xi, tt_, ALU.subtract, ['XI', 'TT'], ['XI'])
            ci = self.CI[si]
            self.tt('dve', XR[:, :, 0:1], XR[:, :, 0:1], ci[:, 0, 4 * j:4 * j + 4].unsqueeze(2), ALU.add,
                    ['XR', ('CI', si)], ['XR'])
            self.tt('dve', XI[:, :, 0:1], XI[:, :, 0:1], ci[:, 1, 4 * j:4 * j + 4].unsqueeze(2), ALU.add,
                    ['XI', ('CI', si)], ['XI'])
            if len(segs) == 1 and nch == NCHMAX and m == NCHMAX:
                fl = lambda t_: t_[:].rearrange("p r c -> p (r c)")
                dec = pr[:, 2, :, :].rearrange("p r c -> p (r c)")
                for (Q, X, qn, xn) in ((QR, XR, 'QR', 'XR'), (QI, XI, 'QI', 'XI')):
                    q_ = fl(Q); x_ = fl(X)
                    self.add('dve', lambda g, q_=q_, x_=x_, dec=dec: g.tensor_tensor_scan(
                        out=q_, data0=dec, data1=x_, initial=0.0, op0=ALU.mult, op1=ALU.add),
                        [xn, prk], [qn])
            else:
                for r in range(4):
                    rho = self.RHO.ap[:, 4 * j + r:4 * j + r + 1].to_broadcast([128, m])
                    for (Q, X, qn, xn) in ((QR, XR, 'QR', 'XR'), (QI, XI, 'QI', 'XI')):
                        q_ = Q[:, r, 0:m]; x_ = X[:, r, 0:m]
                        self.add('dve', lambda g, q_=q_, x_=x_, rho=rho: g.tensor_tensor_scan(
                            out=q_, data0=rho, data1=x_, initial=0.0, op0=ALU.mult, op1=ALU.add),
                            [xn, self.RHO.key], [qn])
            ql = self.QL[si]
            self.cp('dve', ql[:, 0, 4 * j:4 * j + 4], QR[:, :, m - 1], ['QR'], [('QL', si)])
            self.cp('dve', ql[:, 1, 4 * j:4 * j + 4], QI[:, :, m - 1], ['QI'], [('QL', si)])
            if prepass:
                continue
            H = self.HIN[st]
            self.cp('dve', hp[:, 0, :, ca], H[:, 0, 4 * j:4 * j + 4], [('HIN', st)], [hpk])
            self.cp('dve', hp[:, 1, :, ca], H[:, 1, 4 * j:4 * j + 4], [('HIN', st)], [hpk])
            if m > 1:
                mm1 = m - 1
                Rr1 = pr[:, 0, :, 0:mm1]; Ri1 = pr[:, 1, :, 0:mm1]
                qr = QR[:, :, 0:mm1]; qi = QI[:, :, 0:mm1]
                t1 = TT[:, :, 0:mm1]; t2 = self.TT2[:, :, 0:mm1]
                self.tt('dve', t1, qr, Rr1, ALU.mult, ['QR', prk], ['TT'])
                self.tt('dve', t2, qi, Ri1, ALU.mult, ['QI', prk], ['TT2'])
                self.tt('dve', hp[:, 0, :, ca + 1:cb], t1, t2, ALU.subtract, ['TT', 'TT2'], [hpk])
                self.tt('dve', t1, qi, Rr1, ALU.mult, ['QI', prk], ['TT'])
                self.tt('dve', t2, qr, Ri1, ALU.mult, ['QR', prk], ['TT2'])
                self.tt('dve', hp[:, 1, :, ca + 1:cb], t1, t2, ALU.add, ['TT', 'TT2'], [hpk])
        if prepass or self.ssm_lim <= 2:
            return None
        return dict(n=n, nch=nch, KTv=KTv, W3v=W3v, uv=uv, pkk=pkk, ukey=ukey, hp=hp, hpk=hpk)

    def ssm_part2(self, c):
        n = c['n']; nch = c['nch']; KTv = c['KTv']; W3v = c['W3v']; uv = c['uv']
        pkk = c['pkk']; ukey = c['ukey']; hp = c['hp']; hpk = c['hpk']
        pi3, pY = self.ps_get()
        yv = pY[:, 0:n].rearrange("p (c s) -> p s c", s=8)
        first = True
        for sp in range(8):
            for s in range(sp + 1):
                self.mm(yv[:, sp, :], KTv[:, sp - s, :], uv[:, s, :], first, False,
                        [pkk, ukey], [('ps', pi3)], skip_group_check=True)
                first = False
        for sp in range(8 if self.ssm_lim >= 4 else 0):
            for ri in range(2):
                for r in range(4):
                    last = (r == 3 and sp == 7 and ri == 1)
                    self.mm(yv[32 * r:32 * r + 32, sp, :], W3v[:, r, ri, sp, :], hp[:, ri, r, 0:nch], False, last,
                            [pkk, hpk], [('ps', pi3)], tile_position=(0, 32 * r), skip_group_check=True)
        return (pi3, pY)

    def gelu_to(self, out, pY, pkey, n, okey):
        pt = pY
        if self.gelu_native:
            self.actf(out, pt[:, 0:n], AF.Gelu_apprx_tanh, [pkey, 'gateM'], [okey])
            return
        a, ak = self.rg('TF')
        self.actf(a[:, 0:n], pt[:, 0:n], AF.Square, [pkey], [ak])
        self.ts('dve', a[:, 0:n], a[:, 0:n], 0.044715, 1.0, ALU.mult, ALU.add, [ak], [ak])
        self.tt('dve', a[:, 0:n], pt[:, 0:n], a[:, 0:n], ALU.mult, [pkey, ak], [ak])
        b, bk = self.rg('TF')
        self.actf(b[:, 0:n], a[:, 0:n], AF.Sigmoid, [ak], [bk], scale=1.5957691216057308)
        self.tt('dve', out, pt[:, 0:n], b[:, 0:n], ALU.mult, [pkey, bk, 'gateM'], [okey])

    def mkeys(self):
        return ([('OUTA', j) for j in range(8)] + [('YB', j) for j in range(8)] +
                [('OUTB', j) for j in range(8)] + [('MB', j) for j in range(16)])

    def mixer(self, i):
        n = TILE_COLS[i]
        segs = tile_segs(i)
        self.add('dve', lambda g: g.memset(self.gdum[:], 0.0), [], [('ACTB', j) for j in range(FT)] + ['gateM', 'W1ALL'])
        prm = self.prm
        hn = self.hn_rhs(n)
        last_tile = (i == 4)
        if self.lim <= 0:
            return
        for j in range(8):
            pc = self.gemm('c%d' % j, 16, hn, n)
            ph = self.gemm('h%d' % j, 16, hn, n)
            pb = self.gemm('b%d' % j, 16, hn, n)
            tf, tk = self.rg('TF')
            self.cp('act', tf[:, 0:n], pc[1][:, 0:n], [('ps', pc[0])], [tk])
            ch, chk = self.rg('CH')
            t, tk2 = self.rg('TF')
            for si, (c0, c1, st) in enumerate(segs):
                m = c1 - c0
                o = c0 + 2 * (si + 1)
                hk = ('CHH', st, j)
                self.cp('dve', ch[:, o - 2:o], self.CHH[st][:, j, :], [hk], [chk])
                self.tt('dve', ch[:, o:o + m], ph[1][:, c0:c1], tf[:, c0:c1], ALU.mult,
                        [('ps', ph[0]), tk], [chk])
                self.cp('dve', self.CHH[st][:, j, :], ch[:, o + m - 2:o + m], [chk], [hk])
                if last_tile:
                    self.tt('dve', self.CHO[st][:, j, :], ph[1][:, c1 - 2:c1], tf[:, c1 - 2:c1], ALU.mult,
                            [('ps', ph[0]), tk], ['STO'])
                cw = lambda q: prm[:, 48 + 3 * j + q:49 + 3 * j + q]
                self.actf(t[:, c0:c1], ch[:, o - 2:o - 2 + m], AF.Identity, [chk, 'prm'], [tk2], scale=cw(0))
                self.stt(t[:, c0:c1], ch[:, o - 1:o - 1 + m], cw(1), t[:, c0:c1], ALU.mult, ALU.add, [chk, tk2], [tk2])
                self.stt(t[:, c0:c1], ch[:, o:o + m], cw(2), t[:, c0:c1], ALU.mult, ALU.add, [chk, tk2], [tk2])
            self.tt('dve', self.OUTA[:, j, 0:n], pb[1][:, 0:n], t[:, 0:n], ALU.mult, [('ps', pb[0]), tk2, 'gateM'], [('OUTA', j)])
        if self.lim <= 1:
            return
        self.ssm_begin(segs, False)
        def part2(j, ctx):
            pi3, pY = self.ssm_part2(ctx)
            self.gelu_to(self.YB[:, j, 0:n], pY, ('ps', pi3), n, ('YB', j))
            if self.dbg and i == self.dbg_tile:
                self.dbg_dump_ps(j, pY, ('ps', pi3), n, None, None)
        prev = None
        for j in range(8):
            pu = self.gemm('u%d' % j, 16, hn, n)
            u, uk = self.rg('U')
            self.cp('act', u[:, 0:n], pu[1][:, 0:n], [('ps', pu[0])], [uk])
            ctx = self.ssm_tile(j, u, uk, n, segs, False)
            if prev is not None:
                part2(*prev)
            prev = (j, ctx) if ctx is not None else None
        if prev is not None:
            part2(*prev)
        self.ssm_end(segs, False)
        if self.lim <= 2:
            return
        yb = lambda k: (self.YB[:, k, 0:n], ('YB', k))
        for jo in range(8):
            pg = self.gemm('glu%d' % jo, 8, yb, n)
            sg, sk = self.rg('TB')
            self.actf(sg[:, 0:n], pg[1][:, 0:n], AF.Sigmoid, [('ps', pg[0]), 'prm'], [sk], bias=prm[:, 72 + jo:73 + jo])
            self.tt('dve', self.OUTB[:, jo, 0:n], self.YB[:, jo, 0:n], sg[:, 0:n], ALU.mult, [('YB', jo), sk, 'gateM'], [('OUTB', jo)])
        if self.lim <= 3:
            return
        oa = lambda k: (self.OUTA[:, k, 0:n], ('OUTA', k))
        ob = lambda k: (self.OUTB[:, k, 0:n], ('OUTB', k))
        for jo in range(16):
            pa = self.gemm('pa%d' % jo, 8, oa, n)
            pb = self.gemm('pb%d' % jo, 8, ob, n)
            ga = self.gemm('ga%d' % jo, 16, hn, n)
            gb = self.gemm('gb%d' % jo, 16, hn, n)
            sa, sak = self.rg('TF')
            sb_, sbk = self.rg('TF')
            self.actf(sa[:, 0:n], ga[1][:, 0:n], AF.Sigmoid, [('ps', ga[0])], [sak])
            self.actf(sb_[:, 0:n], gb[1][:, 0:n], AF.Sigmoid, [('ps', gb[0])], [sbk])
            self.tt('dve', sa[:, 0:n], pa[1][:, 0:n], sa[:, 0:n], ALU.mult, [('ps', pa[0]), sak], [sak])
            self.tt('dve', sb_[:, 0:n], pb[1][:, 0:n], sb_[:, 0:n], ALU.mult, [('ps', pb[0]), sbk], [sbk])
            self.tt('dve', self.MB[:, jo, 0:n], sa[:, 0:n], sb_[:, 0:n], ALU.add, [sak, sbk, 'gateM'], [('MB', jo)])
        mb = lambda k: (self.MB[:, k, 0:n], ('MB', k))
        for jo in range(16):
            po = self.gemm('wo%d' % jo, 16, mb, n)
            self.tt('dve', self.X[:, jo, 0:n], po[1][:, 0:n], self.X[:, jo, 0:n], ALU.add,
                    [('ps', po[0]), self.xk(jo)], [self.xk(jo)])

    def ffn(self, i):
        n = TILE_COLS[i]
        segs = tile_segs(i)
        self.add('dve', lambda g: g.memset(self.gdum[:], 0.0), [], self.mkeys() + ['gateF'])
        prm = self.prm
        hn = self.hn_rhs(n)
        for j in range(FT):
            res = []
            for (nm, f) in (('ug%d' % j, j), ('uv%d' % j, FT + j)):
                pu = self.gemm(nm, 16, hn, n)
                pk_ = ('ps', pu[0])
                raw, rk = self.rg('RAW')
                t, tk = self.rg('TF')
                fw = lambda q: prm[:, 80 + 3 * f + q:81 + 3 * f + q]
                fb = prm[:, 344 + f:345 + f]
                for si, (c0, c1, st) in enumerate(segs):
                    m = c1 - c0
                    o = c0 + 2 * (si + 1)
                    hk = ('UPH', st, f)
                    self.cp('act', raw[:, o:o + m], pu[1][:, c0:c1], [pk_], [rk])
                    self.cp('dve', raw[:, o - 2:o], self.UPH[st][:, f, :], [hk], [rk])
                    self.actf(t[:, c0:c1], raw[:, o - 2:o - 2 + m], AF.Identity, [rk, 'prm'], [tk], scale=fw(0), bias=fb)
                    self.stt(t[:, c0:c1], raw[:, o - 1:o - 1 + m], fw(1), t[:, c0:c1], ALU.mult, ALU.add, [rk, tk], [tk])
                    self.stt(t[:, c0:c1], pu[1][:, c0:c1], fw(2), t[:, c0:c1], ALU.mult, ALU.add, [pk_, tk], [tk])
                    self.cp('dve', self.UPH[st][:, f, :], raw[:, o + m - 2:o + m], [rk], [hk, 'STO'])
                res.append((t, tk))
            (tg, tgk), (tv, tvk) = res
            self.actf(tg[:, 0:n], tg[:, 0:n], AF.Silu, [tgk], [tgk])
            self.tt('dve', self.ACTB[:, j, 0:n], tg[:, 0:n], tv[:, 0:n], ALU.mult, [tgk, tvk, 'gateF'], [('ACTB', j)])
    def ffn_down(self, i):
        n = TILE_COLS[i]
        ab = lambda k: (self.ACTB[:, k, 0:n], ('ACTB', k))
        for jo in range(16):
            pd = self.gemm('wd%d' % jo, FT, ab, n)
            self.tt('dve', self.X[:, jo, 0:n], pd[1][:, 0:n], self.X[:, jo, 0:n], ALU.add,
                    [('ps', pd[0]), self.xk(jo)], [self.xk(jo)])

    def final(self, i):
        n = TILE_COLS[i]; off = TILE_OFF[i]
        yv = self.d['yT'].rearrange("(k p) n -> p k n", p=128)

        def f(k):
            yo, yk = self.rg('YO')
            self.stt(yo[:, 0:n], self.X[:, k, 0:n], self.prm[:, 32 + k:33 + k], self.RSTD[:, 0:n],
                     ALU.mult, ALU.mult, [self.xk(k), self.rkey, 'prm'], [yk])
            self.stores.append(self.dma('act', 'st', yv[:, k, off:off + n], yo[:, 0:n], [yk], []))
        self.norm(n, f)

    def prepass_bufs(self, i):
        n = 432
        if i % 2 == 0:
            hv = lambda k: self.HN[:, k, 0:n]
            hk = lambda k: ('HN', k)
        else:
            pkb = self.rings_sb['PK']
            hv = lambda k: pkb[k // 8][:, (k % 8) * 432:(k % 8) * 432 + n]
            hk = lambda k: ('PK', k // 8)
        return hv, hk

    def prepass_head(self, i):
        n = 432
        self.set_x(i + NPT)
        self.RSTD = self.RSTDB[i % 2]
        self.rkey = 'RSTD%d' % (i % 2)
        hv, hk = self.prepass_bufs(i)
        self.norm_stats(n)
        for k in range(16):
            self.actf(hv(k), self.X[:, k, 0:n], AF.Identity, [self.xk(k), 'prm'], [hk(k)],
                      scale=self.prm[:, k:k + 1])

    def prepass_tile(self, i):
        n = 432
        segs = [(0, n, 'p')]
        hv, hk = self.prepass_bufs(i)
        rstd = self.RSTDB[i % 2]
        rkey = 'RSTD%d' % (i % 2)
        hn = lambda k: (hv(k), hk(k))

        def uevac(ii, j):
            hv2, hk2 = self.prepass_bufs(ii)
            pu = self.gemm('u%d' % j, 16, lambda k: (hv2(k), hk2(k)), n)
            u, uk = self.rg('U')
            self.tt('dve', u[:, 0:n], pu[1][:, 0:n], self.RSTDB[ii % 2][:, 0:n], ALU.mult,
                    [('ps', pu[0]), 'RSTD%d' % (ii % 2)], [uk])
            return u, uk
        self.ssm_begin(segs, True)
        for j in range(8):
            if j == 0 and self.pre_u0 is not None:
                u, uk = self.pre_u0
                self.pre_u0 = None
            else:
                u, uk = uevac(i, j)
            self.ssm_tile(j, u, uk, n, segs, True)
            if j == 0 and i + 1 < NPT:
                self.prepass_head(i + 1)
            if j == 6 and i + 3 < NPT:
                self.set_x(i + 3 + NPT)
                self.load_x(i + 3, True)
        if i + 1 < NPT:
            self.pre_u0 = uevac(i + 1, 0)
        self.ssm_end(segs, True)

    def dbg_dump_ps(self, j, pY, pkey, n, u, uk):
        t, tk = self.rg('TF')
        self.cp('dve', t[:, 0:n], pY[:, 0:n], [pkey], [tk])
        self.stores.append(self.dma('act', 'st', self.d['dbg'][:, j * 432:j * 432 + n], t[:, 0:n], [tk], []))

    def build(self, ntiles=5, do_prepass=True, do_ffn=True, gelu_native=False, dbg_tile=0, lim=99, ssm_lim=99):
        nc = self.nc
        S = self.S
        self.gelu_native = gelu_native
        self.lim = lim
        self.ssm_lim = ssm_lim
        self.dbg_tile = dbg_tile
        self.declare()
        d = self.d
        es = self.es
        sbp = lambda n, s, t: self.sb(es, n, s, t)
        S.ring('w', NB); S.ring('x', 4); S.ring('pk', 4); S.ring('cv', 8); S.ring('st', 6)
        S.ring('misc', 8)
        self.prm = sbp('prm_s', [128, NPRM], F32)
        self.ctab = sbp('ctab', [128, 11 * 32], F32)
        self.ONES = sbp('ones', [128, 128], BF16)
        self.epsc = sbp('epsc', [128, 1], F32)
        self.STO = sbp('STO', [128, 512], F32)
        self.smp = sbp('smp_s', [128, 256], F32)
        self.CHHt = sbp('CHH', [128, 2, 8, 2], BF16)
        self.TS = sbp('TS', [128, 2, 32], F32)
        self.CIt = sbp('CI', [128, 2, 2, 32], F32)
        self.QLt = sbp('QL', [128, 2, 2, 32], F32)
        self.psg = [es.enter_context(nc.psum_tensor('psg%d' % i, [128, 2048], F32)) for i in range(2)]
        self.ps = [self.psg[i // 4][:, (i % 4) * 512:(i % 4 + 1) * 512] for i in range(8)]
        self.ACTB = sbp('ACTB', [128, FT, NTMAX], BF16)
        self.W1ALL = self.ACTB[:].rearrange("p f n -> p (f n)")[:, 0:8 * 2048].rearrange("p (j e) -> p j e", j=8)
        self.psn = 0
        self.stores = []
        STO = self.STO
        self.CHO = {'p': STO[:, 0:16].rearrange("p (j k) -> p j k", k=2),
                    's': STO[:, 16:32].rearrange("p (j k) -> p j k", k=2)}
        self.UPH = {'p': STO[:, 32:208].rearrange("p (f k) -> p f k", k=2),
                    's': STO[:, 208:384].rearrange("p (f k) -> p f k", k=2)}
        self.HIN = {'p': STO[:, 384:448].rearrange("p (a c) -> p a c", a=2),
                    's': STO[:, 448:512].rearrange("p (a c) -> p a c", a=2)}
        self.CHH = {'p': self.CHHt[:, 0, :, :], 's': self.CHHt[:, 1, :, :]}
        self.CI = [self.CIt[:, 0, :, :], self.CIt[:, 1, :, :]]
        self.QL = [self.QLt[:, 0, :, :], self.QLt[:, 1, :, :]]
        self.dma('sp', 'misc', self.prm[:], d['prm'], [], ['prm'])
        self.dma('sp', 'misc', self.smp[:], d['smp'], [], ['smp'])
        self.add('dve', lambda g: g.memset(self.ONES[:], 1.0), [], ['ONES'])
        self.add('dve', lambda g: g.memset(self.epsc[:], EPS), [], ['epsc'])
        self.add('dve', lambda g: g.memset(STO[:], 0.0), [], ['STO', ('HIN', 'p'), ('HIN', 's')])
        self.add('dve', lambda g: g.memset(self.CHHt[:], 0.0), [], ['CHHt'])
        self.convert_weights()
        self.prologue()
        self.XB = [sbp('X%d' % i, [128, 16, NTMAX], F32) for i in range(2)]
        self.xi = 0
        self.X = self.XB[0]
        self.HN = sbp('HN', [128, 16, NTMAX], BF16)
        self.WR = [sbp('WR%d' % i, [128, SLAB], BF16) for i in range(NB)]
        self.OUTA = self.ACTB[:, 0:8, :]
        self.YB = self.ACTB[:, 8:16, :]
        self.OUTB = self.ACTB[:, 16:24, :]
        self.MB = self.ACTB[:, 24:40, :]
        self.gdum = sbp('gdum', [128, 1], F32)
        self.RSTDB = [sbp('RSTD%d' % i, [128, NTMAX], F32) for i in range(2)]
        self.RSTD = self.RSTDB[0]
        self.rkey = 'RSTD0'
        self.XR = sbp('XR', [128, 4, NCHMAX], F32)
        self.XI = sbp('XI', [128, 4, NCHMAX], F32)
        self.QR = sbp('QR', [128, 4, NCHMAX], F32)
        self.QI = sbp('QI', [128, 4, NCHMAX], F32)
        self.TT = sbp('TT', [128, 4, NCHMAX], F32)
        self.TT2 = sbp('TT2', [128, 4, NCHMAX], F32)
        self.rings_sb = {
            'SQ': [sbp('SQ%d' % i, [128, NTMAX], BF16) for i in range(2)],
            'TF': [sbp('TF%d' % i, [128, NTMAX], F32) for i in range(5)],
            'TB': [sbp('TB%d' % i, [128, NTMAX], BF16) for i in range(2)],
            'RAW': [sbp('RAW%d' % i, [128, NTMAX + 4], F32) for i in range(2)],
            'YO': [sbp('YO%d' % i, [128, NTMAX], F32) for i in range(2)],
            'CH': [sbp('CHb%d' % i, [128, NTMAX + 4], BF16) for i in range(2)],
            'U': [sbp('U%d' % i, [128, NTMAX], BF16) for i in range(3)],
            'PK': [sbp('PK%d' % i, [128, PACK], BF16) for i in range(2)],
            'PR': [sbp('PR%d' % i, [128, 3, 4, NCHMAX], F32) for i in range(2)],
            'HP': [sbp('HP%d' % i, [128, 2, 4, NCHMAX], BF16) for i in range(2)],
        }
        self.ring_n = {}
        self.win = {}
        self.wn = 0
        self.wptr = 0
        pre_slabs = sorted(set(IPOS[x][0] for x in PRE_ITEMS))
        self.wseq = []
        if do_prepass:
            self.wseq += pre_slabs
        last_slab = NSLAB if do_ffn else IPOS['ug0'][0]
        for i in range(ntiles):
            self.wseq += list(range(last_slab))
        smp = self.smp
        self.x_preloaded = False
        self.pre_u0 = None
        if do_prepass:
            self.set_x(NPT)
            self.load_x(0, True)
            self.set_x(NPT + 1)
            self.load_x(1, True)
            self.prepass_head(0)
            self.set_x(NPT + 2)
            self.load_x(2, True)
            for i in range(NPT):
                self.prepass_tile(i)
            self.set_x(0)
            self.RSTD = self.RSTDB[0]
            self.rkey = 'RSTD0'
        self.cp('dve', self.CHH['s'], smp[:, 0:16].rearrange("p (j k) -> p j k", k=2), ['smp', 'CHHt'],
                [('CHH', 's', j) for j in range(8)] + ['CHHt'])
        self.cp('dve', self.UPH['s'], smp[:, 16:192].rearrange("p (f k) -> p f k", k=2), ['smp', 'STO'],
                [('UPH', 's', f) for f in range(2 * FT)])
        self.cp('dve', self.HIN['s'], smp[:, 192:256].rearrange("p (a c) -> p a c", a=2), ['smp', 'STO'],
                [('HIN', 's')])
        for i in range(ntiles):
            self.set_x(i)
            n = TILE_COLS[i]
            if i == 0:
                self.load_x(0)
                self.norm_to_hn(n, 0)
            self.mixer(i)
            if do_ffn:
                self.norm_to_hn(n, 16)
                if i + 1 < ntiles:
                    self.set_x(i + 1)
                    self.load_x(i + 1)
                    self.set_x(i)
                self.ffn(i)
                if i + 1 < ntiles:
                    self.set_x(i + 1)
                    self.norm_to_hn(TILE_COLS[i + 1], 0)
                    self.set_x(i)
                self.ffn_down(i)
            elif i + 1 < ntiles:
                self.set_x(i + 1)
                self.load_x(i + 1)
                self.norm_to_hn(TILE_COLS[i + 1], 0)
                self.set_x(i)
            self.final(i)
        all_sto_keys = ['STO', ('HIN', 'p'), ('HIN', 's')] + [('UPH', st, f) for st in 'ps' for f in range(2 * FT)]
        self.stores.append(self.dma('act', 'st', d['sto'], STO[:], all_sto_keys, []))
        self.add('act', lambda g: g.activation(out=self.epsc[:], in_=self.epsc[:], func=AF.Copy), [], ['epsc'],
                 extra=self.stores)
        engsem = {e: es.enter_context(nc.semaphore('sem_' + e)) for e in Sched.ENG}
        dmasem = {}
        for name, rgd in S.rings.items():
            for i in range(rgd['k']):
                dmasem[(name, i)] = es.enter_context(nc.semaphore('dsem_%s%d' % (name, i)))
        block = es.enter_context(nc.Block())
        S.emit(nc, block, engsem, dmasem)
        es.close()
        return nc


def _fm(v, nt):
    return np.ascontiguousarray(np.asarray(v, np.float32).reshape(nt, 128).T)


def prep_shared(inp):
    f32 = np.float32
    sh = {}
    for nm in ('w_in', 'glu_w', 'proj_a', 'proj_b', 'w_out', 'w_up', 'w_down'):
        sh[nm] = np.ascontiguousarray(np.asarray(inp[nm], f32)[0])
    prm = np.zeros((128, NPRM), f32)
    prm[:, 0:16] = _fm(inp['norm_mix_g'][0], 16)
    prm[:, 16:32] = _fm(inp['norm_ffn_g'][0], 16)
    prm[:, 32:48] = _fm(inp['norm_final_g'], 16)
    cw = np.asarray(inp['conv_a_w'], f32)[0]
    prm[:, 48:72] = cw.reshape(3, 8, 128).transpose(2, 1, 0).reshape(128, 24)
    prm[:, 72:80] = _fm(inp['glu_b'][0], 8)
    fw = np.asarray(inp['ffn_conv_w'], f32)[0]
    prm[:, 80:344] = fw.reshape(3, 88, 128).transpose(2, 1, 0).reshape(128, 264)
    prm[:, 344:432] = _fm(inp['ffn_conv_b'][0], 88)
    prm[:, 432:440] = _fm(np.asarray(inp['ssm_d'], f32)[0].reshape(-1), 8)
    sh['prm'] = prm
    lre = np.asarray(inp['ssm_lambda_re'], f32)[0]
    lim = np.asarray(inp['ssm_lambda_im'], f32)[0]
    ldt = np.broadcast_to(np.asarray(inp['ssm_log_dt'], f32)[0][:, None], (64, 64))
    toA = lambda a: np.ascontiguousarray(a.reshape(32, 2, 64).transpose(1, 2, 0).reshape(128, 32))
    sh['lamA'] = np.ascontiguousarray(np.stack([toA(lre), toA(lim), toA(ldt)], axis=1))
    bre = np.asarray(inp['ssm_b_re'], f32)[0]; bim = np.asarray(inp['ssm_b_im'], f32)[0]
    cre = np.asarray(inp['ssm_c_re'], f32)[0]; cim = np.asarray(inp['ssm_c_im'], f32)[0]

    def layA(x_gph):
        out = np.zeros((2, 64, 32, 2, 16), f32)
        xg = x_gph.reshape(32, 2, 64, 16)
        for g2 in range(2):
            out[g2, :, :, g2, :] = xg[:, g2].transpose(1, 0, 2)
        return out.reshape(128, 1024)
    sh['BA'] = np.ascontiguousarray(np.stack([layA(bre), layA(bim)], axis=1))
    sh['CA'] = np.ascontiguousarray(np.stack([layA(cre.transpose(0, 2, 1)), layA(cim.transpose(0, 2, 1))], axis=1))

    cst = np.zeros((128, 256), f32)
    blk = np.arange(128) // 16
    cst[:, 0:128] = (blk[:, None] == blk[None, :]).astype(f32)
    cst[:, 128:256] = np.eye(128, dtype=f32)
    sh['cst'] = cst
    return sh


def prep_core(inp, c):
    f32 = np.float32
    b, s = c // 4, c % 4
    seq = np.concatenate([np.asarray(inp['meta_tokens'], f32), np.asarray(inp['x_prompt'], f32)[b]], axis=0)
    xt = np.zeros((NTOK, D), f32)
    q0 = SEGLEN * s - 8 - 16
    lo = max(0, -q0)
    xt[lo:NPR] = seq[q0 + lo:q0 + NPR]
    xt[NPR:NTOK] = np.asarray(inp['x_sample'], f32)[c]
    m = {'xT': np.ascontiguousarray(xt.T)}
    xp = np.zeros((NPRE, D), f32)
    p0 = q0 - NPRE
    lo2 = max(0, -p0)
    if lo2 < NPRE:
        xp[lo2:NPRE] = seq[p0 + lo2:p0 + NPRE]
    m['xpre'] = np.ascontiguousarray(xp.T)
    smp = np.zeros((128, 256), f32)
    ca = np.asarray(inp['cache_conv_a'], f32)[0, c]
    smp[:, 0:16] = ca.reshape(2, 8, 128).transpose(2, 1, 0).reshape(128, 16)
    cf = np.asarray(inp['cache_ffn_conv'], f32)[0, c]
    smp[:, 16:192] = cf.reshape(2, 88, 128).transpose(2, 1, 0).reshape(128, 176)
    toA = lambda a: a.reshape(32, 2, 64).transpose(1, 2, 0).reshape(128, 32)
    smp[:, 192:224] = toA(np.asarray(inp['state_ssm_re'], f32)[0, c])
    smp[:, 224:256] = toA(np.asarray(inp['state_ssm_im'], f32)[0, c])
    m['smp'] = smp
    return m


def assemble(res):
    f32 = np.float32
    y_prompt = np.zeros((2, 8192, D), f32)
    y_sample = np.zeros((8, 64, D), f32)
    nca_p = np.zeros((1, 2, 2, 1024), f32); nre_p = np.zeros((1, 2, 64, 64), f32)
    nim_p = np.zeros((1, 2, 64, 64), f32); nff_p = np.zeros((1, 2, 2, 2 * DFF), f32)
    nca_s = np.zeros((1, 8, 2, 1024), f32); nre_s = np.zeros((1, 8, 64, 64), f32)
    nim_s = np.zeros((1, 8, 64, 64), f32); nff_s = np.zeros((1, 8, 2, 2 * DFF), f32)
    fromA = lambda a: a.reshape(2, 64, 32).transpose(2, 0, 1).reshape(64, 64)
    for c in range(8):
        b, s = c // 4, c % 4
        yT = np.asarray(res[c]['yT'])
        sto = np.asarray(res[c]['sto'])
        t0 = SEGLEN * s - 32
        lo = max(0, -t0)
        y_prompt[b, t0 + lo:t0 + SEGLEN] = yT[:, 8 + lo:8 + SEGLEN].T
        y_sample[c] = yT[:, NPR:NTOK].T
        conv = lambda o: sto[:, o:o + 16].reshape(128, 8, 2).transpose(2, 1, 0).reshape(2, 1024)
        ffn = lambda o: sto[:, o:o + 176].reshape(128, 88, 2).transpose(2, 1, 0).reshape(2, 2 * DFF)
        if s == 3:
            nca_p[0, b] = conv(0); nff_p[0, b] = ffn(32)
            nre_p[0, b] = fromA(sto[:, 384:416]); nim_p[0, b] = fromA(sto[:, 416:448])
        nca_s[0, c] = conv(16); nff_s[0, c] = ffn(208)
        nre_s[0, c] = fromA(sto[:, 448:480]); nim_s[0, c] = fromA(sto[:, 480:512])
    return (y_prompt, y_sample, nca_p, nre_p, nim_p, nff_p, nca_s, nre_s, nim_s, nff_s)


def kernel(**inputs):
    sh = prep_shared(inputs)
    in_maps = []
    for c in range(8):
        m = dict(sh)
        m.update(prep_core(inputs, c))
        in_maps.append(m)
    nc = Builder().build()
    res = run_bass_kernel_spmd(nc, in_maps, core_ids=list(range(8)))
    return assemble(res.results)
```
